# Optimizing a Trainium2 kernel written in Bass

```python
import jax, jax.numpy as jnp
from jax import lax
import numpy as np

D_MODEL = 2048
BATCH = 2
SEQ = 16384
DEPTH = 2

POOL_WIDTH = D_MODEL // 2
POOL_WINDOWS = (2, 4, 8, 16)
POOL_GROUP = POOL_WIDTH // len(POOL_WINDOWS)
HEAD_DIM = 64
N_HEADS = D_MODEL // 128
GQA = 8
N_KV = N_HEADS // GQA
WINDOW = 128
BLK = 128
CONV_WIDTH = D_MODEL // 2
CONV_K = 3
D_FF = ((8 * D_MODEL // 3 + 255) // 256) * 256
FFN_CONV_K = 3
N_BRANCH = 3
RMS_EPS = 1e-6
NEG_INF = -1e30

SPLITS = (POOL_WIDTH, N_HEADS * HEAD_DIM, N_KV * HEAD_DIM, N_KV * HEAD_DIM,
          CONV_WIDTH, CONV_WIDTH, CONV_WIDTH, N_BRANCH * D_MODEL)
N_IN = sum(SPLITS)

kernel_name = "hybrid_pool_swa_shortconv_gated"


def rms_norm(x, g):
    x32 = x.astype(jnp.float32)
    y = x32 * lax.rsqrt(jnp.mean(x32 * x32, axis=-1, keepdims=True) + RMS_EPS)
    return (y * g.astype(jnp.float32)).astype(x.dtype)


def causal_dwconv(z, w, b=None):
    K = w.shape[0]
    S = z.shape[1]
    zp = jnp.pad(z, ((0, 0), (K - 1, 0), (0, 0)))
    y = zp[:, 0:S] * w[0]
    for k in range(1, K):
        y = y + zp[:, k:k + S] * w[k]
    if b is not None:
        y = y + b
    return y


def pool_mixer(u, w_grp, scale):
    Bsz, S, C = u.shape
    u32 = u.astype(jnp.float32)
    cs = jnp.concatenate([jnp.zeros((Bsz, 1, C), jnp.float32), jnp.cumsum(u32, axis=1)], axis=1)
    t = jnp.arange(S)
    outs = []
    for gi, w in enumerate(POOL_WINDOWS):
        sl = slice(gi * POOL_GROUP, (gi + 1) * POOL_GROUP)
        csg = cs[..., sl]
        lo = jnp.maximum(t + 1 - w, 0)
        win_sum = csg[:, t + 1] - csg[:, lo]
        cnt = jnp.minimum(t + 1, w).astype(jnp.float32)[None, :, None]
        outs.append(win_sum / cnt - u32[..., sl])
    p = jnp.stack(outs, axis=2).astype(u.dtype)
    y = jnp.einsum('bsgc,gcd->bsgd', p, w_grp).reshape(Bsz, S, C)
    return y * scale


def alibi_slopes():
    h = jnp.arange(1, N_HEADS + 1, dtype=jnp.float32)
    return jnp.exp2(-8.0 * h / N_HEADS).reshape(N_KV, GQA)


def sliding_window_attention(q, k, v, sinks):
    Bsz, S, _ = q.shape
    NB = S // BLK
    q = q.reshape(Bsz, NB, BLK, N_KV, GQA, HEAD_DIM)
    k = k.reshape(Bsz, S, N_KV, HEAD_DIM)
    v = v.reshape(Bsz, S, N_KV, HEAD_DIM)

    def band(z):
        prev = jnp.pad(z, ((0, 0), (BLK, 0), (0, 0), (0, 0)))[:, :S]
        return jnp.concatenate([prev.reshape(Bsz, NB, BLK, N_KV, HEAD_DIM),
                                z.reshape(Bsz, NB, BLK, N_KV, HEAD_DIM)], axis=2)

    kb, vb = band(k), band(v)
    s = jnp.einsum('bnqkgd,bnjkd->bnkgqj', q, kb).astype(jnp.float32) * (HEAD_DIM ** -0.5)
    i = jnp.arange(BLK)[:, None]
    j = jnp.arange(2 * BLK)[None, :]
    dist = i + BLK - j
    key_pos = jnp.arange(NB)[:, None, None] * BLK - BLK + j[None]
    valid = (dist >= 0) & (dist < WINDOW) & (key_pos >= 0)
    s = s - alibi_slopes()[:, :, None, None] * dist.astype(jnp.float32)
    s = jnp.where(valid[None, :, None, None], s, NEG_INF)
    sink = sinks.astype(jnp.float32).reshape(N_KV, GQA)[:, :, None, None]
    m = jnp.maximum(jnp.max(s, axis=-1, keepdims=True), sink)
    p = jnp.exp(s - m)
    denom = jnp.sum(p, axis=-1, keepdims=True) + jnp.exp(sink - m)
    p = (p / denom).astype(v.dtype)
    o = jnp.einsum('bnkgqj,bnjkd->bnqkgd', p, vb)
    return o.reshape(Bsz, S, N_HEADS * HEAD_DIM)


def setup_inputs(seed: int = 0) -> dict:
    key = jax.random.key(seed)
    ks = jax.random.split(key, 20)
    f32 = jnp.float32
    L, D = DEPTH, D_MODEL
    nrm = lambda k, shape, s: jax.random.normal(k, shape, f32) * s
    return {
        "x": jax.random.normal(ks[0], (BATCH, SEQ, D), f32),
        "g_mix": 1.0 + nrm(ks[1], (L, D), 0.05),
        "w_in": nrm(ks[2], (L, D, N_IN), D ** -0.5),
        "b_gate": nrm(ks[3], (L, N_BRANCH * D), 0.02),
        "w_pool_grp": nrm(ks[4], (L, len(POOL_WINDOWS), POOL_GROUP, POOL_GROUP), POOL_GROUP ** -0.5),
        "pool_scale": 1.0 + nrm(ks[5], (L, POOL_WIDTH), 0.1),
        "w_pool_out": nrm(ks[6], (L, POOL_WIDTH, D), POOL_WIDTH ** -0.5),
        "sinks": nrm(ks[7], (L, N_HEADS), 0.5),
        "w_attn_out": nrm(ks[8], (L, N_HEADS * HEAD_DIM, D), (N_HEADS * HEAD_DIM) ** -0.5),
        "w_conv_mix": nrm(ks[9], (L, CONV_K, CONV_WIDTH), CONV_K ** -0.5),
        "w_conv_out": nrm(ks[10], (L, CONV_WIDTH, D), CONV_WIDTH ** -0.5),
        "w_o": nrm(ks[11], (L, D, D), D ** -0.5),
        "g_ffn": 1.0 + nrm(ks[12], (L, D), 0.05),
        "w_up": nrm(ks[13], (L, D, 2 * D_FF), D ** -0.5),
        "w_ffn_conv": nrm(ks[14], (L, FFN_CONV_K, 2 * D_FF), FFN_CONV_K ** -0.5),
        "b_ffn_conv": nrm(ks[15], (L, 2 * D_FF), 0.02),
        "w_down": nrm(ks[16], (L, D_FF, D), D_FF ** -0.5),
        "g_final": 1.0 + nrm(ks[17], (D,), 0.05),
    }


def reference(x, g_mix, w_in, b_gate, w_pool_grp, pool_scale, w_pool_out, sinks, w_attn_out,
              w_conv_mix, w_conv_out, w_o, g_ffn, w_up, w_ffn_conv, b_ffn_conv, w_down, g_final):
    split_idx = np.cumsum(SPLITS)[:-1].tolist()
    for l in range(DEPTH):
        h = rms_norm(x, g_mix[l])
        proj = h @ w_in[l]
        u_pool, q, k, v, c_b, c_c, c_h, gate_logits = jnp.split(proj, split_idx, axis=-1)
        gates = jax.nn.sigmoid((gate_logits + b_gate[l]).astype(jnp.float32)).astype(x.dtype)
        g_a, g_b, g_c = jnp.split(gates, N_BRANCH, axis=-1)
        y_a = pool_mixer(u_pool, w_pool_grp[l], pool_scale[l]) @ w_pool_out[l]
        y_b = sliding_window_attention(q, k, v, sinks[l]) @ w_attn_out[l]
        y_c = (c_b * causal_dwconv(c_c * c_h, w_conv_mix[l])) @ w_conv_out[l]
        merged = g_a * y_a + g_b * y_b + g_c * y_c
        x = x + merged @ w_o[l]
        h2 = rms_norm(x, g_ffn[l])
        u = causal_dwconv(h2 @ w_up[l], w_ffn_conv[l], b_ffn_conv[l])
        u_gate, u_val = jnp.split(u, 2, axis=-1)
        x = x + (jax.nn.silu(u_gate) * u_val) @ w_down[l]
    return rms_norm(x, g_final)
```

```python
import numpy as np
import ml_dtypes
import concourse.bass as bass
import concourse.mybir as mybir
from concourse.bass_utils import run_bass_kernel_spmd

F32 = mybir.dt.float32
BF16 = mybir.dt.bfloat16
ALU = mybir.AluOpType
AF = mybir.ActivationFunctionType

D = 2048
NCH = 16
T = 512
NTILE = 9
NTOK = NTILE * T
SEQ = 16384
DEPTH = 2
N_IN = 11520
DFF = 5632
NA = 44
EPS = 1e-6
NSLOT = 3
LPV = 480
NPV = 2 * LPV + 16

C_POOL, C_Q, C_K, C_V, C_CB, C_CC, C_CH, C_G = 0, 1024, 2048, 2176, 2304, 3328, 4352, 5376


class Res:
    __slots__ = ("w", "r")

    def __init__(self):
        self.w = None
        self.r = {}


class Eng:
    def __init__(self, nc, eng, name, stack, is_pe=False, compute=True):
        self.eng = eng
        self.name = name
        self.is_pe = is_pe
        self.compute = compute
        self.sem = stack.enter_context(nc.semaphore("sem_" + name)) if compute else None
        self.count = 0
        self.seen = {}


class DmaSem:
    def __init__(self, nc, name, stack):
        self.sem = stack.enter_context(nc.semaphore(name))
        self.count = 0


def build_program(ntile=NTILE):
    from contextlib import ExitStack
    nc = bass.Bass("TRN2", target_bir_lowering=False)
    dram = {}

    def din(name, shape, dt=F32):
        dram[name] = nc.dram_tensor(name, list(shape), dt, kind="ExternalInput").ap()
        return dram[name]

    xT = din("xT", [D, NTOK])
    w_in = din("w_in", [DEPTH, D, N_IN])
    w_pool_grp = din("w_pool_grp", [DEPTH, 4, 256, 256])
    w_pool_out = din("w_pool_out", [DEPTH, 1024, D])
    w_attn_out = din("w_attn_out", [DEPTH, 1024, D])
    w_conv_out = din("w_conv_out", [DEPTH, 1024, D])
    w_o = din("w_o", [DEPTH, D, D])
    w_up = din("w_up", [DEPTH, D, 2 * DFF])
    w_down = din("w_down", [DEPTH, DFF, D])
    pvec_d = din("pvec", [128, NPV])
    emat_d = din("emat", [128, 8, 512])
    poolrc_d = din("poolrc", [128, 8, 16])
    vmask_d = din("vmask", [128, 512])
    ident_d = din("ident", [128, 128], BF16)
    outT = nc.dram_tensor("outT", [D, max(ntile - 1, 1) * T], F32, kind="ExternalOutput").ap()

    st = ExitStack()
    with st:
        PE = Eng(nc, nc.tensor, "pe", st, is_pe=True)
        ACT = Eng(nc, nc.scalar, "act", st)
        DVE = Eng(nc, nc.vector, "dve", st)
        SP = Eng(nc, nc.sync, "sp", st, compute=False)
        GQ = Eng(nc, nc.gpsimd, "gq", st, compute=False)
        computes = [PE, ACT, DVE]

        uid = [0]

        def un(name):
            uid[0] += 1
            return f"{name}_{uid[0]}"

        def sb(name, shape, dt):
            return st.enter_context(nc.sbuf_tensor(name, list(shape), dt))

        def do_waits(E, reads, writes):
            need = {}

            def add(ev):
                if ev is None:
                    return
                s, v = ev
                k = id(s)
                if k not in need or need[k][1] < v:
                    need[k] = (s, v)
            for r in reads:
                add(r.w)
            for w in writes:
                add(w.w)
                for ev in w.r.values():
                    add(ev)
            for k, (s, v) in need.items():
                if E.is_pe and s is E.sem:
                    continue
                if E.seen.get(k, 0) < v:
                    E.eng.wait_ge(s, v)
                    E.seen[k] = v

        def op(E, fn, reads=(), writes=(), signal=True):
            do_waits(E, reads, writes)
            ins = fn()
            if signal:
                E.count += 1
                ins.then_inc(E.sem, 1)
                val = E.count
            else:
                val = E.count + 1
            ev = (E.sem, val)
            for r in reads:
                k = id(E.sem)
                if k not in r.r or r.r[k][1] < val:
                    r.r[k] = ev
            for w in writes:
                w.w = ev
                w.r = {}
            return ins

        def dma(Q, dsem, fn, reads=(), writes=()):
            do_waits(Q, reads, writes)
            ins = fn()
            dsem.count += 16
            ins.then_inc(dsem.sem, 16)
            ev = (dsem.sem, dsem.count)
            for r in reads:
                r.r[id(dsem.sem)] = ev
            for w in writes:
                w.w = ev
                w.r = {}

        def fence():
            for E in computes:
                for Fe in computes:
                    if E.is_pe and Fe is E:
                        continue
                    if Fe.count > 0 and E.seen.get(id(Fe.sem), 0) < Fe.count:
                        E.eng.wait_ge(Fe.sem, Fe.count)
                        E.seen[id(Fe.sem)] = Fe.count

        X = sb("X", [128, NCH, T], F32)
        Xr = [Res() for _ in range(NCH)]
        H = sb("H", [128, NCH, T], BF16)
        Hr = [Res() for _ in range(NCH)]
        WS = [sb(f"WS{i}", [128, 8192], BF16) for i in range(NSLOT)]
        WSr = [Res() for _ in range(NSLOT)]
        WSsem = [DmaSem(nc, f"ws{i}", st) for i in range(NSLOT)]
        WG = sb("WG", [128, DEPTH, 4, 2, 256], BF16)
        WGr = Res()
        EM = sb("EM", [128, 8, 512], F32)
        PVt = sb("PVt", [128, NPV], F32)
        RC = sb("RC", [128, 8, 16], F32)
        VM = sb("VM", [128, 512], F32)
        ID = sb("ID", [128, 128], BF16)
        CONSTr = Res()
        HBG = sb("HBG", [128, 96], F32)
        ESC = sb("ESC", [128, 32], F32)
        VCOL = sb("VCOL", [128, 64], BF16)
        ONESB = sb("ONESB", [128, 128], BF16)
        MISCr = Res()
        KB = [sb(f"KB{l}", [128, 2, 640], BF16) for l in range(DEPTH)]
        KBr = [Res() for _ in range(DEPTH)]
        VA = [[[sb(f"VA{l}{kv}{par}", [128, 5, 128], BF16) for par in range(2)] for kv in range(2)] for l in range(DEPTH)]
        VAr = [Res() for _ in range(DEPTH)]
        UC = [sb(f"UC{l}", [128, 8, 15], F32) for l in range(DEPTH)]
        UCr = [Res() for _ in range(DEPTH)]
        CZC = [sb(f"CZC{l}", [128, 8, 2], F32) for l in range(DEPTH)]
        CZCr = [Res() for _ in range(DEPTH)]
        ZC = [sb(f"ZC{l}", [128, NA, 2, 2], F32) for l in range(DEPTH)]
        ZCr = [Res() for _ in range(DEPTH)]
        SQ = [sb(f"SQ{i}", [128, T], BF16) for i in range(2)]
        SQr = [Res() for _ in range(2)]
        RS = sb("RS", [128, T], F32)
        RSr = Res()
        MB = {}

        banks = [st.enter_context(nc.psum_tensor(f"pb{i}", [128, T], F32)) for i in range(7)]
        bankr = [Res() for _ in range(7)]
        PSB = st.enter_context(nc.psum_tensor("psb", [128, 4, 128], BF16))
        PSBr = Res()
        bank_ctr = [0]

        def new_bank():
            i = bank_ctr[0] % 7
            bank_ctr[0] += 1
            return banks[i][:, :], bankr[i]

        csem = DmaSem(nc, "csem", st)
        xsem = DmaSem(nc, "xsem", st)
        osem = DmaSem(nc, "osem", st)

        slot_ctr = [0]

        def load_slab(parts):
            s = slot_ctr[0] % NSLOT
            slot_ctr[0] += 1
            for dv, src in parts:
                dma(GQ, WSsem[s], lambda dv=dv, src=src: nc.gpsimd.dma_start(out=dv(WS[s]), in_=src), writes=[WSr[s]])
            return s

        def wview(s, nk, n):
            return WS[s][:, 0:nk * n].rearrange("p (k n) -> p k n", n=n)

        def slab_cols(wl, nk, c0, ncol):
            src = wl.rearrange("(k p) n -> p k n", p=128)[:, :, c0:c0 + ncol]
            return load_slab([(lambda t: t[:, 0:nk * ncol].rearrange("p (k n) -> p k n", n=ncol), src)])

        def mm_group(bank, bres, lhs_list, rhs_list, rres, extra_reads=()):
            n = len(lhs_list)
            for k in range(n):
                rd = [rres[k]] + list(extra_reads)
                op(PE, lambda k=k: nc.tensor.matmul(bank, lhsT=lhs_list[k], rhs=rhs_list[k], start=(k == 0), stop=(k == n - 1)),
                   reads=rd, writes=[bres], signal=(k == n - 1))

        dma(SP, csem, lambda: nc.sync.dma_start(out=PVt[:], in_=pvec_d[:, :]), writes=[CONSTr])
        dma(SP, csem, lambda: nc.sync.dma_start(out=EM[:], in_=emat_d[:, :, :]), writes=[CONSTr])
        dma(SP, csem, lambda: nc.sync.dma_start(out=RC[:], in_=poolrc_d[:, :, :]), writes=[CONSTr])
        dma(SP, csem, lambda: nc.sync.dma_start(out=VM[:], in_=vmask_d[:, :]), writes=[CONSTr])
        dma(SP, csem, lambda: nc.sync.dma_start(out=ID[:], in_=ident_d[:, :]), writes=[CONSTr])
        for l in range(DEPTH):
            src = w_pool_grp[l].rearrange("g (kc p) d -> p g kc d", p=128)
            dma(GQ, csem, lambda l=l, src=src: nc.gpsimd.dma_start(out=WG[:, l], in_=src), writes=[WGr])
        for l in range(DEPTH):
            op(DVE, lambda l=l: nc.vector.tensor_scalar(out=HBG[:, l * 48:(l + 1) * 48], in0=PVt[:, l * LPV + 32:l * LPV + 80],
                                                        scalar1=0.5, scalar2=None, op0=ALU.mult), reads=[CONSTr], writes=[MISCr])
            op(ACT, lambda l=l: nc.scalar.activation(out=ESC[:, l * 16:(l + 1) * 16], in_=PVt[:, l * LPV + 464:l * LPV + 480], func=AF.Exp),
               reads=[CONSTr], writes=[MISCr])
        op(DVE, lambda: nc.vector.tensor_copy(out=VCOL[:], in_=VM[:, 0:64]), reads=[CONSTr], writes=[MISCr])
        op(DVE, lambda: nc.vector.memset(ONESB[:], 1.0), writes=[MISCr])
        for l in range(DEPTH):
            op(DVE, lambda l=l: nc.vector.memset(KB[l][:], 0.0), writes=[KBr[l]])
            for kv in range(2):
                for par in range(2):
                    op(DVE, lambda l=l, kv=kv, par=par: nc.vector.memset(VA[l][kv][par][:], 1.0), writes=[VAr[l]])
            op(DVE, lambda l=l: nc.vector.memset(UC[l][:], 0.0), writes=[UCr[l]])
            op(DVE, lambda l=l: nc.vector.memset(CZC[l][:], 0.0), writes=[CZCr[l]])
            op(DVE, lambda l=l: nc.vector.memset(ZC[l][:], 0.0), writes=[ZCr[l]])
        fence()

        def pcol(c):
            return PVt[:, c:c + 1]

        def norm_stage(gbase, to_x=False):
            bank, bres = new_bank()
            for c in range(NCH):
                i = c % 2
                op(ACT, lambda c=c, i=i: nc.scalar.activation(out=SQ[i][:], in_=X[:, c, :], func=AF.Square), reads=[Xr[c]], writes=[SQr[i]])
                op(PE, lambda c=c, i=i: nc.tensor.matmul(bank, lhsT=ONESB[:], rhs=SQ[i][:], start=(c == 0), stop=(c == NCH - 1)),
                   reads=[SQr[i], MISCr], writes=[bres], signal=True)
            op(ACT, lambda: nc.scalar.activation(out=RS[:], in_=bank, func=AF.Sqrt, bias=EPS, scale=1.0 / D), reads=[bres], writes=[RSr])
            op(DVE, lambda: nc.vector.reciprocal(out=RS[:], in_=RS[:]), reads=[RSr], writes=[RSr])
            for c in range(NCH):
                if to_x:
                    op(DVE, lambda c=c: nc.vector.scalar_tensor_tensor(out=X[:, c, :], in0=X[:, c, :], scalar=pcol(gbase + c), in1=RS[:],
                                                                       op0=ALU.mult, op1=ALU.mult), reads=[Xr[c], RSr, CONSTr], writes=[Xr[c]])
                else:
                    op(DVE, lambda c=c: nc.vector.scalar_tensor_tensor(out=H[:, c, :], in0=X[:, c, :], scalar=pcol(gbase + c), in1=RS[:],
                                                                       op0=ALU.mult, op1=ALU.mult), reads=[Xr[c], RSr, CONSTr], writes=[Hr[c]])

        def proj_slab(l, c0, consume):
            s = slab_cols(w_in[l], 16, c0, 512)
            wv = wview(s, 16, 512)
            for j in range(4):
                bank, bres = new_bank()
                mm_group(bank, bres, [wv[:, k, j * 128:(j + 1) * 128] for k in range(16)], [H[:, k, :] for k in range(16)], Hr, extra_reads=[WSr[s]])
                consume(j, bank, bres)

        def pool_stage(l, ti):
            pb = l * LPV
            PA, PAr, OT, OTr, CC, CCr = MB['PA'], MB['PAr'], MB['OT'], MB['OTr'], MB['CC'], MB['CCr']
            with nc.sbuf_tensor(un("U"), [128, 8, 527], F32) as U, nc.sbuf_tensor(un("PW0"), [128, 2, 527], F32) as PW0, \
                    nc.sbuf_tensor(un("PW1"), [128, 2, 527], F32) as PW1, nc.sbuf_tensor(un("P"), [128, 8, T], BF16) as P, \
                    nc.sbuf_tensor(un("PFX"), [128, 2, 16], F32) as PFX:
                Ur = [Res() for _ in range(8)]
                Uh = Res()
                PWr = [Res(), Res()]
                Pr = [Res() for _ in range(8)]
                PFXr = Res()
                op(ACT, lambda: nc.scalar.copy(out=U[:, :, 0:15], in_=UC[l][:]), reads=[UCr[l]], writes=[Uh])
                for half in range(2):
                    def cons(j, bank, bres, half=half):
                        c = half * 4 + j
                        op(ACT, lambda: nc.scalar.copy(out=U[:, c, 15:527], in_=bank), reads=[bres], writes=[Ur[c]])
                    proj_slab(l, C_POOL + half * 512, cons)
                PW = [PW0, PW1]
                for g in range(4):
                    w = 2 << g
                    pr = [Ur[2 * g], Ur[2 * g + 1], Uh]
                    up = U[:, 2 * g:2 * g + 2, :]
                    op(DVE, lambda up=up: nc.vector.tensor_tensor(out=PW0[:, :, 1:527], in0=up[:, :, 1:527], in1=up[:, :, 0:526], op=ALU.add),
                       reads=pr, writes=[PWr[0]])
                    cur = 0
                    lo = 1
                    sh = 2
                    while sh < w:
                        nlo = lo + sh
                        src = PW[cur]
                        dst = PW[1 - cur]
                        op(DVE, lambda src=src, dst=dst, nlo=nlo, sh=sh: nc.vector.tensor_tensor(out=dst[:, :, nlo:527], in0=src[:, :, nlo:527],
                                                                                                in1=src[:, :, nlo - sh:527 - sh], op=ALU.add),
                           reads=[PWr[cur]], writes=[PWr[1 - cur]])
                        cur = 1 - cur
                        lo = nlo
                        sh *= 2
                    S = PW[cur]
                    op(DVE, lambda S=S, up=up, g=g, w=w: nc.vector.scalar_tensor_tensor(out=P[:, 2 * g:2 * g + 2, :], in0=S[:, :, 15:527], scalar=1.0 / w,
                                                                                        in1=up[:, :, 15:527], op0=ALU.mult, op1=ALU.subtract),
                       reads=[PWr[cur]] + pr, writes=[Pr[2 * g], Pr[2 * g + 1]])
                    if ti == 1:
                        op(DVE, lambda S=S, g=g: nc.vector.tensor_tensor(out=PFX[:], in0=S[:, :, 15:31], in1=RC[:, 2 * g:2 * g + 2, :], op=ALU.mult),
                           reads=[PWr[cur], CONSTr], writes=[PFXr])
                        op(DVE, lambda up=up, g=g: nc.vector.tensor_tensor(out=P[:, 2 * g:2 * g + 2, 0:16], in0=PFX[:], in1=up[:, :, 15:31], op=ALU.subtract),
                           reads=[PFXr] + pr, writes=[Pr[2 * g], Pr[2 * g + 1]])
                op(ACT, lambda: nc.scalar.copy(out=UC[l][:], in_=U[:, :, 512:527]), reads=Ur, writes=[UCr[l]])
                for g in range(4):
                    for mc in range(2):
                        bank, bres = new_bank()
                        mm_group(bank, bres, [WG[:, l, g, kc, mc * 128:(mc + 1) * 128] for kc in range(2)],
                                 [P[:, 2 * g + kc, :] for kc in range(2)], [Pr[2 * g], Pr[2 * g + 1]], extra_reads=[WGr])
                        c = 2 * g + mc
                        op(ACT, lambda c=c, bank=bank: nc.scalar.activation(out=PA[:, c, :], in_=bank, func=AF.Identity, scale=pcol(pb + 80 + c)),
                           reads=[bres, CONSTr], writes=[PAr[c]])
                fence()

        def attn_stage(l, ti):
            PA, PAr, OT, OTr, CC, CCr = MB['PA'], MB['PAr'], MB['OT'], MB['OTr'], MB['CC'], MB['CCr']
            with nc.sbuf_tensor(un("Q"), [128, 8, T], BF16) as Q, nc.sbuf_tensor(un("V"), [128, T], BF16) as V, \
                    nc.sbuf_tensor(un("EX"), [128, 4, T], F32) as EX, nc.sbuf_tensor(un("PT"), [128, 4, T], BF16) as PT, \
                    nc.sbuf_tensor(un("RD"), [128, 2, T], F32) as RD:
                Qr = [Res() for _ in range(8)]
                Vr = Res()
                EXr = [Res() for _ in range(4)]
                PTr = [Res() for _ in range(4)]
                RDr = [Res() for _ in range(2)]
                for half in range(2):
                    def cons(j, bank, bres, half=half):
                        c = half * 4 + j
                        op(ACT, lambda: nc.scalar.mul(out=Q[:, c, :], in_=bank, mul=0.125), reads=[bres], writes=[Qr[c]])
                    proj_slab(l, C_Q + half * 512, cons)
                wl = w_in[l].rearrange("(k p) n -> p k n", p=128)

                def kvdst(c0, c1):
                    return lambda t: t[:, 0:16 * 384].rearrange("p (k n) -> p k n", n=384)[:, :, c0:c1]
                s = load_slab([(kvdst(0, 64), wl[:, :, C_K:C_K + 64]), (kvdst(64, 128), wl[:, :, C_K:C_K + 64]),
                               (kvdst(128, 192), wl[:, :, C_K + 64:C_K + 128]), (kvdst(192, 256), wl[:, :, C_K + 64:C_K + 128]),
                               (kvdst(256, 384), wl[:, :, C_V:C_V + 128])])
                kvw = wview(s, 16, 384)
                for kv in range(2):
                    bank, bres = new_bank()
                    mm_group(bank, bres, [kvw[:, k, kv * 128:(kv + 1) * 128] for k in range(16)], [H[:, k, :] for k in range(16)], Hr, extra_reads=[WSr[s]])
                    op(ACT, lambda kv=kv, bank=bank: nc.scalar.copy(out=KB[l][:, kv, 128:640], in_=bank), reads=[bres], writes=[KBr[l]])
                bank, bres = new_bank()
                mm_group(bank, bres, [kvw[:, k, 256:384] for k in range(16)], [H[:, k, :] for k in range(16)], Hr, extra_reads=[WSr[s]])
                op(ACT, lambda bank=bank: nc.scalar.copy(out=V[:], in_=bank), reads=[bres], writes=[Vr])
                for b in range(4):
                    op(PE, lambda b=b: nc.tensor.transpose(out=PSB[:, b, :], in_=V[:, b * 128:(b + 1) * 128], identity=ID[:]),
                       reads=[Vr, CONSTr], writes=[PSBr], signal=(b == 3))
                for kv in range(2):
                    op(DVE, lambda kv=kv: nc.vector.tensor_copy(out=VA[l][kv][0][:, 1:5, 0:64], in_=PSB[:, :, kv * 64:(kv + 1) * 64]),
                       reads=[PSBr], writes=[VAr[l]])
                    op(DVE, lambda kv=kv: nc.vector.tensor_copy(out=VA[l][kv][1][:, 1:5, 64:128], in_=PSB[:, :, kv * 64:(kv + 1) * 64]),
                       reads=[PSBr], writes=[VAr[l]])
                if ti == 0:
                    for kv in range(2):
                        op(DVE, lambda kv=kv: nc.vector.tensor_copy(out=VA[l][kv][0][:, 4, 64:128], in_=VCOL[:]), reads=[MISCr], writes=[VAr[l]])
                        op(DVE, lambda kv=kv: nc.vector.tensor_copy(out=VA[l][kv][1][:, 4, 0:64], in_=VCOL[:]), reads=[MISCr], writes=[VAr[l]])
                combos = [(qb, kv, par) for qb in range(4) for kv in range(2) for par in range(2)]
                state = {}

                def emit_S(ci):
                    qb, kv, par = combos[ci]
                    i0 = (ci % 2) * 2
                    rows = slice(par * 64, (par + 1) * 64)
                    e = (kv * 2 + par) * 2
                    bks = []
                    for w_ in range(2):
                        bank, bres = new_bank()
                        kc0 = (qb + w_) * 128
                        op(PE, lambda bank=bank, kc0=kc0: nc.tensor.matmul(bank, lhsT=KB[l][rows, kv, kc0:kc0 + 128],
                                                                          rhs=Q[rows, kv * 4:kv * 4 + 4, qb * 128:(qb + 1) * 128], start=True, stop=True),
                           reads=[KBr[l]] + Qr[kv * 4:kv * 4 + 4], writes=[bres])
                        bks.append((bank, bres))
                    for w_ in range(2):
                        bank, bres = bks[w_]
                        i = i0 + w_
                        op(ACT, lambda bank=bank, i=i: nc.scalar.activation(out=EX[:, i, :], in_=bank, func=AF.Exp), reads=[bres], writes=[EXr[i]])
                        op(DVE, lambda i=i, w_=w_: nc.vector.tensor_tensor(out=PT[:, i, :], in0=EX[:, i, :], in1=EM[:, e + w_, :], op=ALU.mult),
                           reads=[EXr[i], CONSTr], writes=[PTr[i]])

                def emit_PV(ci):
                    qb, kv, par = combos[ci]
                    i0 = (ci % 2) * 2
                    ri = ci % 2
                    orow = slice(par * 64, (par + 1) * 64)
                    drow = slice((1 - par) * 64, (2 - par) * 64)
                    bank, bres = new_bank()
                    op(PE, lambda: nc.tensor.matmul(bank, lhsT=VA[l][kv][par][:, qb, :], rhs=PT[:, i0, :], start=True, stop=False),
                       reads=[VAr[l], PTr[i0]], writes=[bres], signal=False)
                    op(PE, lambda: nc.tensor.matmul(bank, lhsT=VA[l][kv][par][:, qb + 1, :], rhs=PT[:, i0 + 1, :], start=False, stop=True),
                       reads=[VAr[l], PTr[i0 + 1]], writes=[bres])
                    op(ACT, lambda: nc.scalar.copy(out=RD[orow, ri, :], in_=bank[drow, :]), reads=[bres], writes=[RDr[ri]])
                    for jj in range(4):
                        h = kv * 8 + 2 * jj + par
                        op(DVE, lambda jj=jj, h=h: nc.vector.tensor_scalar(out=RD[orow, ri, jj * 128:(jj + 1) * 128], in0=RD[orow, ri, jj * 128:(jj + 1) * 128],
                                                                           scalar1=ESC[orow, l * 16 + h:l * 16 + h + 1], scalar2=None, op0=ALU.add),
                           reads=[RDr[ri], MISCr], writes=[RDr[ri]])
                    op(DVE, lambda: nc.vector.reciprocal(out=RD[orow, ri, :], in_=RD[orow, ri, :]), reads=[RDr[ri]], writes=[RDr[ri]])
                    op(DVE, lambda: nc.vector.tensor_tensor(out=OT[orow, kv * 4:kv * 4 + 4, qb * 128:(qb + 1) * 128],
                                                            in0=bank[orow, :].rearrange("p (a b) -> p a b", a=4),
                                                            in1=RD[orow, ri, :].rearrange("p (a b) -> p a b", a=4), op=ALU.mult),
                       reads=[bres, RDr[ri]], writes=[OTr])

                emit_S(0)
                for ci in range(len(combos)):
                    if ci + 1 < len(combos):
                        emit_S(ci + 1)
                    emit_PV(ci)
                op(ACT, lambda: nc.scalar.copy(out=KB[l][:, :, 0:128], in_=KB[l][:, :, 512:640]), reads=[KBr[l]], writes=[KBr[l]])
                for kv in range(2):
                    for par in range(2):
                        op(ACT, lambda kv=kv, par=par: nc.scalar.copy(out=VA[l][kv][par][:, 0, :], in_=VA[l][kv][par][:, 4, :]), reads=[VAr[l]], writes=[VAr[l]])
                if ti == 0:
                    for kv in range(2):
                        op(DVE, lambda kv=kv: nc.vector.memset(VA[l][kv][0][:, 4, 64:128], 1.0), reads=[VAr[l]], writes=[VAr[l]])
                        op(DVE, lambda kv=kv: nc.vector.memset(VA[l][kv][1][:, 4, 0:64], 1.0), reads=[VAr[l]], writes=[VAr[l]])
                fence()

        def conv_stage(l, ti):
            pb = l * LPV
            PA, PAr, OT, OTr, CC, CCr = MB['PA'], MB['PAr'], MB['OT'], MB['OTr'], MB['CC'], MB['CCr']
            with nc.sbuf_tensor(un("CCt"), [128, 4, T], F32) as CCt, nc.sbuf_tensor(un("ZZ"), [128, 4, 514], F32) as ZZ:
                for half in range(2):
                    CCtr = [Res() for _ in range(4)]
                    ZZr = [Res() for _ in range(4)]
                    ZZh = Res()
                    op(ACT, lambda: nc.scalar.copy(out=ZZ[:, :, 0:2], in_=CZC[l][:, half * 4:half * 4 + 4, :]), reads=[CZCr[l]], writes=[ZZh])

                    def cons_c(j, bank, bres):
                        op(ACT, lambda: nc.scalar.copy(out=CCt[:, j, :], in_=bank), reads=[bres], writes=[CCtr[j]])
                    proj_slab(l, C_CC + half * 512, cons_c)

                    def cons_h(j, bank, bres):
                        c = half * 4 + j
                        op(DVE, lambda: nc.vector.tensor_tensor(out=ZZ[:, j, 2:514], in0=CCt[:, j, :], in1=bank, op=ALU.mult),
                           reads=[bres, CCtr[j]], writes=[ZZr[j]])
                        op(ACT, lambda: nc.scalar.activation(out=CCt[:, j, :], in_=ZZ[:, j, 0:512], func=AF.Identity, scale=pcol(pb + 88 + c)),
                           reads=[ZZr[j], ZZh, CONSTr], writes=[CCtr[j]])
                        op(DVE, lambda: nc.vector.scalar_tensor_tensor(out=CCt[:, j, :], in0=ZZ[:, j, 1:513], scalar=pcol(pb + 96 + c), in1=CCt[:, j, :],
                                                                       op0=ALU.mult, op1=ALU.add), reads=[ZZr[j], ZZh, CCtr[j], CONSTr], writes=[CCtr[j]])
                        op(DVE, lambda: nc.vector.scalar_tensor_tensor(out=CCt[:, j, :], in0=ZZ[:, j, 2:514], scalar=pcol(pb + 104 + c), in1=CCt[:, j, :],
                                                                       op0=ALU.mult, op1=ALU.add), reads=[ZZr[j], CCtr[j], CONSTr], writes=[CCtr[j]])
                    proj_slab(l, C_CH + half * 512, cons_h)
                    op(ACT, lambda: nc.scalar.copy(out=CZC[l][:, half * 4:half * 4 + 4, :], in_=ZZ[:, :, 512:514]), reads=ZZr, writes=[CZCr[l]])

                    def cons_b(j, bank, bres):
                        c = half * 4 + j
                        op(DVE, lambda: nc.vector.tensor_tensor(out=CC[:, c, :], in0=CCt[:, j, :], in1=bank, op=ALU.mult),
                           reads=[bres, CCtr[j]], writes=[CCr[c]])
                    proj_slab(l, C_CB + half * 512, cons_b)
                    fence()

        def merge_stage(l, ti):
            PA, PAr, OT, OTr, CC, CCr = MB['PA'], MB['PAr'], MB['OT'], MB['OTr'], MB['CC'], MB['CCr']
            with nc.sbuf_tensor(un("MG"), [128, NCH, T], BF16) as MG, nc.sbuf_tensor(un("ACC"), [128, 4, T], F32) as ACC, \
                    nc.sbuf_tensor(un("TH"), [128, 2, T], F32) as TH, nc.sbuf_tensor(un("TMP"), [128, 2, T], F32) as TMP:
                MGr = [Res() for _ in range(NCH)]
                ACCr = [Res() for _ in range(4)]
                THr = [Res(), Res()]
                TMPr = [Res(), Res()]
                srcs = [(PA, PAr, w_pool_out), (OT, [OTr] * 8, w_attn_out), (CC, CCr, w_conv_out)]
                cnt = 0
                for m4 in range(4):
                    for br in range(3):
                        SRC, SRCr, wout = srcs[br]
                        sg = slab_cols(w_in[l], 16, C_G + br * D + m4 * 512, 512)
                        so = slab_cols(wout[l], 8, m4 * 512, 512)
                        wg = wview(sg, 16, 512)
                        wo = wview(so, 8, 512)
                        for mm in range(4):
                            m = m4 * 4 + mm
                            bg, bgr = new_bank()
                            mm_group(bg, bgr, [wg[:, k, mm * 128:(mm + 1) * 128] for k in range(16)], [H[:, k, :] for k in range(16)], Hr, extra_reads=[WSr[sg]])
                            by, byr = new_bank()
                            mm_group(by, byr, [wo[:, k, mm * 128:(mm + 1) * 128] for k in range(8)], [SRC[:, k, :] for k in range(8)], SRCr, extra_reads=[WSr[so]])
                            i = cnt % 2
                            cnt += 1
                            op(ACT, lambda bg=bg, i=i, br=br, m=m: nc.scalar.activation(out=TH[:, i, :], in_=bg, func=AF.Tanh, scale=0.5,
                                                                                        bias=HBG[:, l * 48 + br * 16 + m:l * 48 + br * 16 + m + 1]),
                               reads=[bgr, MISCr], writes=[THr[i]])
                            if br == 0:
                                op(DVE, lambda by=by, i=i, mm=mm: nc.vector.scalar_tensor_tensor(out=ACC[:, mm, :], in0=TH[:, i, :], scalar=1.0, in1=by,
                                                                                                 op0=ALU.add, op1=ALU.mult), reads=[THr[i], byr], writes=[ACCr[mm]])
                            else:
                                op(DVE, lambda by=by, i=i: nc.vector.scalar_tensor_tensor(out=TMP[:, i, :], in0=TH[:, i, :], scalar=1.0, in1=by,
                                                                                          op0=ALU.add, op1=ALU.mult), reads=[THr[i], byr], writes=[TMPr[i]])
                                if br == 1:
                                    op(DVE, lambda i=i, mm=mm: nc.vector.tensor_tensor(out=ACC[:, mm, :], in0=ACC[:, mm, :], in1=TMP[:, i, :], op=ALU.add),
                                       reads=[ACCr[mm], TMPr[i]], writes=[ACCr[mm]])
                                else:
                                    op(DVE, lambda i=i, mm=mm, m=m: nc.vector.tensor_tensor(out=MG[:, m, :], in0=ACC[:, mm, :], in1=TMP[:, i, :], op=ALU.add),
                                       reads=[ACCr[mm], TMPr[i]], writes=[MGr[m]])
                for mq in range(4):
                    s = slab_cols(w_o[l], 16, mq * 512, 512)
                    wv = wview(s, 16, 512)
                    for mm in range(4):
                        m = mq * 4 + mm
                        bank, bres = new_bank()
                        mm_group(bank, bres, [wv[:, k, mm * 128:(mm + 1) * 128] for k in range(16)], [MG[:, k, :] for k in range(16)], MGr, extra_reads=[WSr[s]])
                        op(DVE, lambda bank=bank, m=m: nc.vector.scalar_tensor_tensor(out=X[:, m, :], in0=bank, scalar=0.5, in1=X[:, m, :],
                                                                                      op0=ALU.mult, op1=ALU.add), reads=[bres, Xr[m]], writes=[Xr[m]])
                fence()

        def ffn_stage(l, ti):
            pb = l * LPV
            with nc.sbuf_tensor(un("A0"), [128, 12, T], BF16) as A0, nc.sbuf_tensor(un("A1"), [128, 12, T], BF16) as A1, \
                    nc.sbuf_tensor(un("ZB"), [128, 2, 2, 514], F32) as ZB, nc.sbuf_tensor(un("CGV"), [128, 2, 2, T], F32) as CGV, \
                    nc.sbuf_tensor(un("TH2"), [128, 2, T], F32) as TH, nc.sbuf_tensor(un("TMP2"), [128, 2, T], F32) as TMP:
                Ab = [A0, A1]
                Abr = [[Res() for _ in range(12)] for _ in range(2)]
                ZBr = [Res(), Res()]
                CGVr = [Res(), Res()]
                THr = [Res(), Res()]
                TMPr = [Res(), Res()]
                groups = [(0, 12), (12, 12), (24, 12), (36, 8)]
                for gi, (g0, ng) in enumerate(groups):
                    ab = Ab[gi % 2]
                    abr = Abr[gi % 2]
                    for i4 in range(g0 // 4, (g0 + ng) // 4):
                        sg = slab_cols(w_up[l], 16, i4 * 512, 512)
                        sv = slab_cols(w_up[l], 16, DFF + i4 * 512, 512)
                        wg = wview(sg, 16, 512)
                        wv = wview(sv, 16, 512)
                        for jj in range(4):
                            j = i4 * 4 + jj
                            jl = j - g0
                            i = j % 2
                            bg, bgr = new_bank()
                            mm_group(bg, bgr, [wg[:, k, jj * 128:(jj + 1) * 128] for k in range(16)], [H[:, k, :] for k in range(16)], Hr, extra_reads=[WSr[sg]])
                            bv, bvr = new_bank()
                            mm_group(bv, bvr, [wv[:, k, jj * 128:(jj + 1) * 128] for k in range(16)], [H[:, k, :] for k in range(16)], Hr, extra_reads=[WSr[sv]])
                            op(DVE, lambda i=i, j=j: nc.vector.tensor_copy(out=ZB[:, i, :, 0:2], in_=ZC[l][:, j, :, :]), reads=[ZCr[l]], writes=[ZBr[i]])
                            op(ACT, lambda i=i, bg=bg: nc.scalar.copy(out=ZB[:, i, 0, 2:514], in_=bg), reads=[bgr, ZBr[i]], writes=[ZBr[i]])
                            op(ACT, lambda i=i, bv=bv: nc.scalar.copy(out=ZB[:, i, 1, 2:514], in_=bv), reads=[bvr, ZBr[i]], writes=[ZBr[i]])
                            op(DVE, lambda i=i, j=j: nc.vector.tensor_copy(out=ZC[l][:, j, :, :], in_=ZB[:, i, :, 512:514]), reads=[ZBr[i]], writes=[ZCr[l]])
                            for s_ in range(2):
                                cj = s_ * NA + j
                                op(ACT, lambda i=i, s_=s_, cj=cj: nc.scalar.activation(out=CGV[:, i, s_, :], in_=ZB[:, i, s_, 0:512], func=AF.Identity,
                                                                                       scale=pcol(pb + 112 + cj), bias=pcol(pb + 376 + cj)),
                                   reads=[ZBr[i], CONSTr], writes=[CGVr[i]])
                                op(DVE, lambda i=i, s_=s_, cj=cj: nc.vector.scalar_tensor_tensor(out=CGV[:, i, s_, :], in0=ZB[:, i, s_, 1:513], scalar=pcol(pb + 112 + 88 + cj),
                                                                                                 in1=CGV[:, i, s_, :], op0=ALU.mult, op1=ALU.add),
                                   reads=[ZBr[i], CGVr[i], CONSTr], writes=[CGVr[i]])
                                op(DVE, lambda i=i, s_=s_, cj=cj: nc.vector.scalar_tensor_tensor(out=CGV[:, i, s_, :], in0=ZB[:, i, s_, 2:514], scalar=pcol(pb + 112 + 176 + cj),
                                                                                                 in1=CGV[:, i, s_, :], op0=ALU.mult, op1=ALU.add),
                                   reads=[ZBr[i], CGVr[i], CONSTr], writes=[CGVr[i]])
                            op(ACT, lambda i=i: nc.scalar.activation(out=TH[:, i, :], in_=CGV[:, i, 0, :], func=AF.Tanh, scale=0.5), reads=[CGVr[i]], writes=[THr[i]])
                            op(DVE, lambda i=i: nc.vector.scalar_tensor_tensor(out=TMP[:, i, :], in0=TH[:, i, :], scalar=1.0, in1=CGV[:, i, 0, :],
                                                                               op0=ALU.add, op1=ALU.mult), reads=[THr[i], CGVr[i]], writes=[TMPr[i]])
                            op(DVE, lambda i=i, jl=jl, ab=ab: nc.vector.scalar_tensor_tensor(out=ab[:, jl, :], in0=TMP[:, i, :], scalar=0.5, in1=CGV[:, i, 1, :],
                                                                                             op0=ALU.mult, op1=ALU.mult), reads=[TMPr[i], CGVr[i]], writes=[abr[jl]])
                    for mq in range(4):
                        src = w_down[l][g0 * 128:(g0 + ng) * 128, :].rearrange("(k p) n -> p k n", p=128)[:, :, mq * 512:(mq + 1) * 512]
                        s = load_slab([(lambda t, ng=ng: t[:, 0:ng * 512].rearrange("p (k n) -> p k n", n=512), src)])
                        wv = wview(s, ng, 512)
                        for mm in range(4):
                            m = mq * 4 + mm
                            bank, bres = new_bank()
                            mm_group(bank, bres, [wv[:, k, mm * 128:(mm + 1) * 128] for k in range(ng)], [ab[:, k, :] for k in range(ng)], abr[:ng], extra_reads=[WSr[s]])
                            op(DVE, lambda bank=bank, m=m: nc.vector.tensor_tensor(out=X[:, m, :], in0=bank, in1=X[:, m, :], op=ALU.add),
                               reads=[bres, Xr[m]], writes=[Xr[m]])
                fence()

        xv = xT.rearrange("(c p) t -> p c t", p=128)
        ov = outT.rearrange("(c p) t -> p c t", p=128)
        for ti in range(ntile):
            dma(SP, xsem, lambda ti=ti: nc.sync.dma_start(out=X[:], in_=xv[:, :, ti * T:(ti + 1) * T]), writes=Xr)
            for l in range(DEPTH):
                pb = l * LPV
                norm_stage(pb + 0)
                with nc.sbuf_tensor(un("PA"), [128, 8, T], BF16) as PA_, nc.sbuf_tensor(un("OT"), [128, 8, T], BF16) as OT_, \
                        nc.sbuf_tensor(un("CC"), [128, 8, T], BF16) as CC_:
                    MB.update(PA=PA_, OT=OT_, CC=CC_, PAr=[Res() for _ in range(8)], OTr=Res(), CCr=[Res() for _ in range(8)])
                    pool_stage(l, ti)
                    attn_stage(l, ti)
                    conv_stage(l, ti)
                    merge_stage(l, ti)
                norm_stage(pb + 16)
                ffn_stage(l, ti)
                if ti == 0 and l == 0:
                    for c in range(NCH):
                        op(DVE, lambda c=c: nc.vector.tensor_tensor(out=X[:, c, :], in0=X[:, c, :], in1=VM[:], op=ALU.mult),
                           reads=[Xr[c], CONSTr], writes=[Xr[c]])
            if ti >= 1:
                norm_stage(2 * LPV, to_x=True)
                dma(SP, osem, lambda ti=ti: nc.sync.dma_start(out=ov[:, :, (ti - 1) * T:ti * T], in_=X[:]), reads=Xr)
        nc.sync.wait_ge(osem.sem, osem.count)
        fence()
    return nc


_NT = NTILE


def _consts():
    em = np.zeros((128, 8, 4, 128), np.float64)
    j = np.arange(128)[:, None]
    i = np.arange(128)[None, :]
    for kv in range(2):
        for par in range(2):
            for jj in range(4):
                h = kv * 8 + 2 * jj + par
                slope = 2.0 ** (-8.0 * (h + 1) / 16.0)
                for w_ in range(2):
                    dist = (i + 128 - j) if w_ == 0 else (i - j)
                    valid = (dist >= 0) & (dist < 128)
                    em[:, (kv * 2 + par) * 2 + w_, jj, :] = np.where(valid, np.exp(-slope * dist), 0.0)
    return em.reshape(128, 8, 512).astype(np.float32)


def kernel(x, g_mix, w_in, b_gate, w_pool_grp, pool_scale, w_pool_out, sinks, w_attn_out,
           w_conv_mix, w_conv_out, w_o, g_ffn, w_up, w_ffn_conv, b_ffn_conv, w_down, g_final):
    f = lambda a: np.ascontiguousarray(np.asarray(a, dtype=np.float32))
    x = f(x)
    pv = np.zeros((128, NPV), np.float32)

    def cols(v):
        return np.asarray(v, np.float32).reshape(-1, 128).T
    for l in range(DEPTH):
        b = l * LPV
        pv[:, b:b + 16] = cols(g_mix[l])
        pv[:, b + 16:b + 32] = cols(g_ffn[l])
        pv[:, b + 32:b + 80] = cols(b_gate[l])
        pv[:, b + 80:b + 88] = cols(pool_scale[l])
        for k in range(3):
            pv[:, b + 88 + k * 8:b + 96 + k * 8] = cols(w_conv_mix[l][k])
            pv[:, b + 112 + k * 88:b + 200 + k * 88] = cols(w_ffn_conv[l][k])
        pv[:, b + 376:b + 464] = cols(b_ffn_conv[l])
        pv[:, b + 464:b + 480] = np.broadcast_to(np.asarray(sinks[l], np.float32)[None, :], (128, 16))
    pv[:, 2 * LPV:2 * LPV + 16] = cols(g_final)
    emat = _consts()
    ident = np.eye(128, dtype=np.float32).astype(ml_dtypes.bfloat16)
    wts = dict(w_in=f(w_in), w_pool_grp=f(w_pool_grp), w_pool_out=f(w_pool_out), w_attn_out=f(w_attn_out),
               w_conv_out=f(w_conv_out), w_o=f(w_o), w_up=f(w_up), w_down=f(w_down))
    in_maps = []
    for c in range(8):
        b, q = c // 4, c % 4
        a = q * 4096
        xt = np.zeros((D, NTOK), np.float32)
        lo = a - T
        if lo < 0:
            xt[:, T:] = x[b, 0:4096, :].T
        else:
            xt[:, :] = x[b, lo:a + 4096, :].T
        rc = np.zeros((128, 8, 16), np.float32)
        for g in range(4):
            w = 2 << g
            if q == 0:
                rc[:, 2 * g:2 * g + 2, :] = (1.0 / np.minimum(np.arange(16) + 1, w)).astype(np.float32)[None, None, :]
            else:
                rc[:, 2 * g:2 * g + 2, :] = np.float32(1.0 / w)
        vm = np.full((128, 512), 0.0 if q == 0 else 1.0, np.float32)
        m = dict(xT=xt, pvec=pv, emat=emat, poolrc=rc, vmask=vm, ident=ident)
        m.update(wts)
        in_maps.append(m)
    nc = build_program(_NT)
    res = run_bass_kernel_spmd(nc, in_maps, core_ids=list(range(8)))
    out = np.zeros((2, SEQ, D), np.float32)
    n = (_NT - 1) * T
    for c in range(8):
        b, q = c // 4, c % 4
        out[b, q * 4096:q * 4096 + n, :] = res.results[c]["outT"].T
    return out
```

```python
import numpy as np
import ml_dtypes
import concourse.bass as bass
import concourse.mybir as mybir
from concourse.bass_utils import run_bass_kernel_spmd

F32 = mybir.dt.float32
BF16 = mybir.dt.bfloat16
ALU = mybir.AluOpType
AF = mybir.ActivationFunctionType

D = 2048
NCH = 16
T = 512
NTILE = 9
NTOK = NTILE * T
SEQ = 16384
DEPTH = 2
N_IN = 11520
DFF = 5632
NA = 44
EPS = 1e-6
NSLOT = 3
LPV = 480
NPV = 2 * LPV + 16

C_POOL, C_Q, C_K, C_V, C_CB, C_CC, C_CH, C_G = 0, 1024, 2048, 2176, 2304, 3328, 4352, 5376


class Res:
    __slots__ = ("w", "r")

    def __init__(self):
        self.w = None
        self.r = {}


class Eng:
    def __init__(self, nc, eng, name, stack, is_pe=False, compute=True):
        self.eng = eng
        self.name = name
        self.is_pe = is_pe
        self.compute = compute
        self.sem = stack.enter_context(nc.semaphore("sem_" + name)) if compute else None
        self.count = 0
        self.seen = {}


class DmaSem:
    def __init__(self, nc, name, stack):
        self.sem = stack.enter_context(nc.semaphore(name))
        self.count = 0


def build_program(ntile=NTILE):
    from contextlib import ExitStack
    nc = bass.Bass("TRN2", target_bir_lowering=False)
    dram = {}

    def din(name, shape, dt=F32):
        dram[name] = nc.dram_tensor(name, list(shape), dt, kind="ExternalInput").ap()
        return dram[name]

    xT = din("xT", [D, NTOK])
    w_in = din("w_in", [DEPTH, D, N_IN])
    w_pool_grp = din("w_pool_grp", [DEPTH, 4, 256, 256])
    w_pool_out = din("w_pool_out", [DEPTH, 1024, D])
    w_attn_out = din("w_attn_out", [DEPTH, 1024, D])
    w_conv_out = din("w_conv_out", [DEPTH, 1024, D])
    w_o = din("w_o", [DEPTH, D, D])
    w_up = din("w_up", [DEPTH, D, 2 * DFF])
    w_down = din("w_down", [DEPTH, DFF, D])
    pvec_d = din("pvec", [128, NPV])
    emat_d = din("emat", [128, 8, 512])
    poolrc_d = din("poolrc", [128, 8, 16])
    vmask_d = din("vmask", [128, 512])
    ident_d = din("ident", [128, 128], BF16)
    outT = nc.dram_tensor("outT", [D, max(ntile - 1, 1) * T], F32, kind="ExternalOutput").ap()

    st = ExitStack()
    with st:
        PE = Eng(nc, nc.tensor, "pe", st, is_pe=True)
        ACT = Eng(nc, nc.scalar, "act", st)
        DVE = Eng(nc, nc.vector, "dve", st)
        SP = Eng(nc, nc.sync, "sp", st, compute=False)
        GQ = Eng(nc, nc.gpsimd, "gq", st, compute=False)
        computes = [PE, ACT, DVE]

        uid = [0]

        def un(name):
            uid[0] += 1
            return f"{name}_{uid[0]}"

        def sb(name, shape, dt):
            return st.enter_context(nc.sbuf_tensor(name, list(shape), dt))

        def do_waits(E, reads, writes):
            need = {}

            def add(ev):
                if ev is None:
                    return
                s, v = ev
                k = id(s)
                if k not in need or need[k][1] < v:
                    need[k] = (s, v)
            for r in reads:
                add(r.w)
            for w in writes:
                add(w.w)
                for ev in w.r.values():
                    add(ev)
            for k, (s, v) in need.items():
                if E.is_pe and s is E.sem:
                    continue
                if E.seen.get(k, 0) < v:
                    E.eng.wait_ge(s, v)
                    E.seen[k] = v

        def op(E, fn, reads=(), writes=(), signal=True):
            do_waits(E, reads, writes)
            ins = fn()
            if signal:
                E.count += 1
                ins.then_inc(E.sem, 1)
                val = E.count
            else:
                val = E.count + 1
            ev = (E.sem, val)
            for r in reads:
                k = id(E.sem)
                if k not in r.r or r.r[k][1] < val:
                    r.r[k] = ev
            for w in writes:
                w.w = ev
                w.r = {}
            return ins

        def dma(Q, dsem, fn, reads=(), writes=(), nowait=False):
            if not nowait:
                do_waits(Q, reads, writes)
            ins = fn()
            dsem.count += 16
            ins.then_inc(dsem.sem, 16)
            ev = (dsem.sem, dsem.count)
            for r in reads:
                r.r[id(dsem.sem)] = ev
            for w in writes:
                w.w = ev
                w.r = {}

        def fence():
            for E in computes:
                for Fe in computes:
                    if E.is_pe and Fe is E:
                        continue
                    if Fe.count > 0 and E.seen.get(id(Fe.sem), 0) < Fe.count:
                        E.eng.wait_ge(Fe.sem, Fe.count)
                        E.seen[id(Fe.sem)] = Fe.count

        grave = {}

        def mk(L):
            r = Res()
            r.r = dict(grave)
            L.append(r)
            return r

        def bury(L):
            for r in L:
                evs = list(r.r.values()) + ([r.w] if r.w else [])
                for (s_, v) in evs:
                    k = id(s_)
                    if k not in grave or grave[k][1] < v:
                        grave[k] = (s_, v)
            del L[:]

        X = sb("X", [128, NCH, T], F32)
        Xr = [Res() for _ in range(NCH)]
        H = sb("H", [128, NCH, T], BF16)
        Hr = [Res() for _ in range(NCH)]
        WS = [sb(f"WS{i}", [128, 8192], BF16) for i in range(NSLOT)]
        WSr = [Res() for _ in range(NSLOT)]
        WSsem = [DmaSem(nc, f"ws{i}", st) for i in range(NSLOT)]
        WG = sb("WG", [128, DEPTH, 4, 2, 256], BF16)
        WGr = Res()
        EM = sb("EM", [128, 8, 512], F32)
        PVt = sb("PVt", [128, NPV], F32)
        RC = sb("RC", [128, 8, 16], F32)
        VM = sb("VM", [128, 512], F32)
        ID = sb("ID", [128, 128], BF16)
        CONSTr = Res()
        HBG = sb("HBG", [128, 96], F32)
        ESC = sb("ESC", [128, 32], F32)
        VCOL = sb("VCOL", [128, 64], BF16)
        ONESB = sb("ONESB", [128, 128], BF16)
        MISCr = Res()
        KB = [sb(f"KB{l}", [128, 2, 640], BF16) for l in range(DEPTH)]
        KBr = [Res() for _ in range(DEPTH)]
        VA = [[[sb(f"VA{l}{kv}{par}", [128, 5, 128], BF16) for par in range(2)] for kv in range(2)] for l in range(DEPTH)]
        VAr = [Res() for _ in range(DEPTH)]
        UC = [sb(f"UC{l}", [128, 8, 15], F32) for l in range(DEPTH)]
        UCr = [Res() for _ in range(DEPTH)]
        CZC = [sb(f"CZC{l}", [128, 8, 2], F32) for l in range(DEPTH)]
        CZCr = [Res() for _ in range(DEPTH)]
        ZC = [sb(f"ZC{l}", [128, NA, 2, 2], F32) for l in range(DEPTH)]
        ZCr = [Res() for _ in range(DEPTH)]
        SQ = [sb(f"SQ{i}", [128, T], BF16) for i in range(2)]
        SQr = [Res() for _ in range(2)]
        RS = sb("RS", [128, T], F32)
        RSr = Res()
        MB = {}

        banks = [st.enter_context(nc.psum_tensor(f"pb{i}", [128, T], F32)) for i in range(7)]
        bankr = [Res() for _ in range(7)]
        PSB = st.enter_context(nc.psum_tensor("psb", [128, 4, 128], BF16))
        PSBr = Res()
        bank_ctr = [0]

        def new_bank():
            i = bank_ctr[0] % 7
            bank_ctr[0] += 1
            return banks[i][:, :], bankr[i]

        csem = DmaSem(nc, "csem", st)
        xsem = [DmaSem(nc, f"xsem{c}", st) for c in range(NCH)]
        osem = [DmaSem(nc, f"osem{c}", st) for c in range(NCH)]

        slot_ctr = [0]

        def load_slab(parts):
            s = slot_ctr[0] % NSLOT
            slot_ctr[0] += 1
            for pi, (dv, src) in enumerate(parts):
                dma(GQ, WSsem[s], lambda dv=dv, src=src: nc.gpsimd.dma_start(out=dv(WS[s]), in_=src), writes=[WSr[s]], nowait=(pi > 0))
            return s

        def wview(s, nk, n):
            return WS[s][:, 0:nk * n].rearrange("p (k n) -> p k n", n=n)

        def slab_cols(wl, nk, c0, ncol):
            src = wl.rearrange("(k p) n -> p k n", p=128)[:, :, c0:c0 + ncol]
            return load_slab([(lambda t: t[:, 0:nk * ncol].rearrange("p (k n) -> p k n", n=ncol), src)])

        def mm_group(bank, bres, lhs_list, rhs_list, rres, extra_reads=()):
            n = len(lhs_list)
            for k in range(n):
                rd = [rres[k]] + list(extra_reads)
                op(PE, lambda k=k: nc.tensor.matmul(bank, lhsT=lhs_list[k], rhs=rhs_list[k], start=(k == 0), stop=(k == n - 1)),
                   reads=rd, writes=[bres], signal=(k == n - 1))

        dma(SP, csem, lambda: nc.sync.dma_start(out=PVt[:], in_=pvec_d[:, :]), writes=[CONSTr])
        dma(SP, csem, lambda: nc.sync.dma_start(out=EM[:], in_=emat_d[:, :, :]), writes=[CONSTr])
        dma(SP, csem, lambda: nc.sync.dma_start(out=RC[:], in_=poolrc_d[:, :, :]), writes=[CONSTr])
        dma(SP, csem, lambda: nc.sync.dma_start(out=VM[:], in_=vmask_d[:, :]), writes=[CONSTr])
        dma(SP, csem, lambda: nc.sync.dma_start(out=ID[:], in_=ident_d[:, :]), writes=[CONSTr])
        for l in range(DEPTH):
            src = w_pool_grp[l].rearrange("g (kc p) d -> p g kc d", p=128)
            dma(GQ, csem, lambda l=l, src=src: nc.gpsimd.dma_start(out=WG[:, l], in_=src), writes=[WGr])
        for l in range(DEPTH):
            op(DVE, lambda l=l: nc.vector.tensor_scalar(out=HBG[:, l * 48:(l + 1) * 48], in0=PVt[:, l * LPV + 32:l * LPV + 80],
                                                        scalar1=0.5, scalar2=None, op0=ALU.mult), reads=[CONSTr], writes=[MISCr])
            op(ACT, lambda l=l: nc.scalar.activation(out=ESC[:, l * 16:(l + 1) * 16], in_=PVt[:, l * LPV + 464:l * LPV + 480], func=AF.Exp),
               reads=[CONSTr], writes=[MISCr])
        op(DVE, lambda: nc.vector.tensor_copy(out=VCOL[:], in_=VM[:, 0:64]), reads=[CONSTr], writes=[MISCr])
        op(DVE, lambda: nc.vector.memset(ONESB[:], 1.0), writes=[MISCr])
        for l in range(DEPTH):
            op(DVE, lambda l=l: nc.vector.memset(KB[l][:], 0.0), writes=[KBr[l]])
            for kv in range(2):
                for par in range(2):
                    op(DVE, lambda l=l, kv=kv, par=par: nc.vector.memset(VA[l][kv][par][:], 1.0), writes=[VAr[l]])
            op(DVE, lambda l=l: nc.vector.memset(UC[l][:], 0.0), writes=[UCr[l]])
            op(DVE, lambda l=l: nc.vector.memset(CZC[l][:], 0.0), writes=[CZCr[l]])
            op(DVE, lambda l=l: nc.vector.memset(ZC[l][:], 0.0), writes=[ZCr[l]])
        fence()

        def pcol(c):
            return PVt[:, c:c + 1]

        def norm_stage(gbase, to_x=False, cs=slice(0, T), store_ti=None):
            bank, bres = new_bank()
            bcs = slice(0, cs.stop - cs.start)
            for c in range(NCH):
                i = c % 2
                op(ACT, lambda c=c, i=i: nc.scalar.activation(out=SQ[i][:, cs], in_=X[:, c, cs], func=AF.Square), reads=[Xr[c]], writes=[SQr[i]])
                op(PE, lambda c=c, i=i: nc.tensor.matmul(bank[:, bcs], lhsT=ONESB[:], rhs=SQ[i][:, cs], start=(c == 0), stop=(c == NCH - 1)),
                   reads=[SQr[i], MISCr], writes=[bres], signal=True)
            op(ACT, lambda: nc.scalar.activation(out=RS[:, cs], in_=bank[:, bcs], func=AF.Sqrt, bias=EPS, scale=1.0 / D), reads=[bres], writes=[RSr])
            op(DVE, lambda: nc.vector.reciprocal(out=RS[:, cs], in_=RS[:, cs]), reads=[RSr], writes=[RSr])
            for c in range(NCH):
                if to_x:
                    op(DVE, lambda c=c: nc.vector.scalar_tensor_tensor(out=X[:, c, cs], in0=X[:, c, cs], scalar=pcol(gbase + c), in1=RS[:, cs],
                                                                       op0=ALU.mult, op1=ALU.mult), reads=[Xr[c], RSr, CONSTr], writes=[Xr[c]])
                    if store_ti is not None:
                        dma(SP, osem[c], lambda c=c: nc.sync.dma_start(out=ov[:, c, (store_ti - 1) * T:store_ti * T], in_=X[:, c, :]), reads=[Xr[c]])
                else:
                    op(DVE, lambda c=c: nc.vector.scalar_tensor_tensor(out=H[:, c, cs], in0=X[:, c, cs], scalar=pcol(gbase + c), in1=RS[:, cs],
                                                                       op0=ALU.mult, op1=ALU.mult), reads=[Xr[c], RSr, CONSTr], writes=[Hr[c]])

        def proj_slab(l, c0, consume):
            s = slab_cols(w_in[l], 16, c0, 512)
            wv = wview(s, 16, 512)
            for j in range(4):
                bank, bres = new_bank()
                mm_group(bank, bres, [wv[:, k, j * 128:(j + 1) * 128] for k in range(16)], [H[:, k, :] for k in range(16)], Hr, extra_reads=[WSr[s]])
                consume(j, bank, bres)

        def pool_stage(l, ti):
            pb = l * LPV
            PA, PAr, OT, OTr, CC, CCr = MB['PA'], MB['PAr'], MB['OT'], MB['OTr'], MB['CC'], MB['CCr']
            L = []
            with nc.sbuf_tensor(un("U"), [128, 8, 527], F32) as U, nc.sbuf_tensor(un("PW0"), [128, 2, 527], F32) as PW0, \
                    nc.sbuf_tensor(un("PW1"), [128, 2, 527], F32) as PW1, nc.sbuf_tensor(un("P"), [128, 8, T], BF16) as P, \
                    nc.sbuf_tensor(un("PFX"), [128, 2, 16], F32) as PFX:
                Ur = [mk(L) for _ in range(8)]
                Uh = mk(L)
                PWr = [mk(L), mk(L)]
                Pr = [mk(L) for _ in range(8)]
                PFXr = mk(L)
                op(ACT, lambda: nc.scalar.copy(out=U[:, :, 0:15], in_=UC[l][:]), reads=[UCr[l]], writes=[Uh])
                for half in range(2):
                    def cons(j, bank, bres, half=half):
                        c = half * 4 + j
                        op(ACT, lambda: nc.scalar.copy(out=U[:, c, 15:527], in_=bank), reads=[bres], writes=[Ur[c]])
                    proj_slab(l, C_POOL + half * 512, cons)
                PW = [PW0, PW1]
                for g in range(4):
                    w = 2 << g
                    pr = [Ur[2 * g], Ur[2 * g + 1], Uh]
                    up = U[:, 2 * g:2 * g + 2, :]
                    op(DVE, lambda up=up: nc.vector.tensor_tensor(out=PW0[:, :, 1:527], in0=up[:, :, 1:527], in1=up[:, :, 0:526], op=ALU.add),
                       reads=pr, writes=[PWr[0]])
                    cur = 0
                    lo = 1
                    sh = 2
                    while sh < w:
                        nlo = lo + sh
                        src = PW[cur]
                        dst = PW[1 - cur]
                        op(DVE, lambda src=src, dst=dst, nlo=nlo, sh=sh: nc.vector.tensor_tensor(out=dst[:, :, nlo:527], in0=src[:, :, nlo:527],
                                                                                                in1=src[:, :, nlo - sh:527 - sh], op=ALU.add),
                           reads=[PWr[cur]], writes=[PWr[1 - cur]])
                        cur = 1 - cur
                        lo = nlo
                        sh *= 2
                    S = PW[cur]
                    op(DVE, lambda S=S, up=up, g=g, w=w: nc.vector.scalar_tensor_tensor(out=P[:, 2 * g:2 * g + 2, :], in0=S[:, :, 15:527], scalar=1.0 / w,
                                                                                        in1=up[:, :, 15:527], op0=ALU.mult, op1=ALU.subtract),
                       reads=[PWr[cur]] + pr, writes=[Pr[2 * g], Pr[2 * g + 1]])
                    if ti == 1:
                        op(DVE, lambda S=S, g=g: nc.vector.tensor_tensor(out=PFX[:], in0=S[:, :, 15:31], in1=RC[:, 2 * g:2 * g + 2, :], op=ALU.mult),
                           reads=[PWr[cur], CONSTr], writes=[PFXr])
                        op(DVE, lambda up=up, g=g: nc.vector.tensor_tensor(out=P[:, 2 * g:2 * g + 2, 0:16], in0=PFX[:], in1=up[:, :, 15:31], op=ALU.subtract),
                           reads=[PFXr] + pr, writes=[Pr[2 * g], Pr[2 * g + 1]])
                op(ACT, lambda: nc.scalar.copy(out=UC[l][:], in_=U[:, :, 512:527]), reads=Ur, writes=[UCr[l]])
                for g in range(4):
                    for mc in range(2):
                        bank, bres = new_bank()
                        mm_group(bank, bres, [WG[:, l, g, kc, mc * 128:(mc + 1) * 128] for kc in range(2)],
                                 [P[:, 2 * g + kc, :] for kc in range(2)], [Pr[2 * g], Pr[2 * g + 1]], extra_reads=[WGr])
                        c = 2 * g + mc
                        op(ACT, lambda c=c, bank=bank: nc.scalar.activation(out=PA[:, c, :], in_=bank, func=AF.Identity, scale=pcol(pb + 80 + c)),
                           reads=[bres, CONSTr], writes=[PAr[c]])
                bury(L)

        def attn_conv_stage(l, ti):
            pb = l * LPV
            PA, PAr, OT, OTr, CC, CCr = MB['PA'], MB['PAr'], MB['OT'], MB['OTr'], MB['CC'], MB['CCr']
            L = []
            with nc.sbuf_tensor(un("Q"), [128, 8, T], BF16) as Q, nc.sbuf_tensor(un("V"), [128, T], BF16) as V, \
                    nc.sbuf_tensor(un("EX"), [128, 4, T], F32) as EX, nc.sbuf_tensor(un("PT"), [128, 4, T], BF16) as PT, \
                    nc.sbuf_tensor(un("RD"), [128, 2, T], F32) as RD, \
                    nc.sbuf_tensor(un("CCt"), [128, 2, T], F32) as CCt, nc.sbuf_tensor(un("ZZ"), [128, 2, 514], F32) as ZZ:
                Qr = [mk(L) for _ in range(8)]
                Vr = mk(L)
                EXr = [mk(L) for _ in range(4)]
                PTr = [mk(L) for _ in range(4)]
                RDr = [mk(L) for _ in range(2)]
                CCtr = [mk(L) for _ in range(2)]
                ZZr = [mk(L) for _ in range(2)]
                ZZh = mk(L)
                for half in range(2):
                    def cons(j, bank, bres, half=half):
                        c = half * 4 + j
                        op(ACT, lambda: nc.scalar.mul(out=Q[:, c, :], in_=bank, mul=0.125), reads=[bres], writes=[Qr[c]])
                    proj_slab(l, C_Q + half * 512, cons)
                wl = w_in[l].rearrange("(k p) n -> p k n", p=128)

                def kvdst(c0, c1):
                    return lambda t: t[:, 0:16 * 384].rearrange("p (k n) -> p k n", n=384)[:, :, c0:c1]
                s = load_slab([(kvdst(0, 64), wl[:, :, C_K:C_K + 64]), (kvdst(64, 128), wl[:, :, C_K:C_K + 64]),
                               (kvdst(128, 192), wl[:, :, C_K + 64:C_K + 128]), (kvdst(192, 256), wl[:, :, C_K + 64:C_K + 128]),
                               (kvdst(256, 384), wl[:, :, C_V:C_V + 128])])
                kvw = wview(s, 16, 384)
                for kv in range(2):
                    bank, bres = new_bank()
                    mm_group(bank, bres, [kvw[:, k, kv * 128:(kv + 1) * 128] for k in range(16)], [H[:, k, :] for k in range(16)], Hr, extra_reads=[WSr[s]])
                    op(ACT, lambda kv=kv, bank=bank: nc.scalar.copy(out=KB[l][:, kv, 128:640], in_=bank), reads=[bres], writes=[KBr[l]])
                bank, bres = new_bank()
                mm_group(bank, bres, [kvw[:, k, 256:384] for k in range(16)], [H[:, k, :] for k in range(16)], Hr, extra_reads=[WSr[s]])
                op(ACT, lambda bank=bank: nc.scalar.copy(out=V[:], in_=bank), reads=[bres], writes=[Vr])
                for b in range(4):
                    op(PE, lambda b=b: nc.tensor.transpose(out=PSB[:, b, :], in_=V[:, b * 128:(b + 1) * 128], identity=ID[:]),
                       reads=[Vr, CONSTr], writes=[PSBr], signal=(b == 3))
                for kv in range(2):
                    op(DVE, lambda kv=kv: nc.vector.tensor_copy(out=VA[l][kv][0][:, 1:5, 0:64], in_=PSB[:, :, kv * 64:(kv + 1) * 64]),
                       reads=[PSBr], writes=[VAr[l]])
                    op(DVE, lambda kv=kv: nc.vector.tensor_copy(out=VA[l][kv][1][:, 1:5, 64:128], in_=PSB[:, :, kv * 64:(kv + 1) * 64]),
                       reads=[PSBr], writes=[VAr[l]])
                if ti == 0:
                    for kv in range(2):
                        op(DVE, lambda kv=kv: nc.vector.tensor_copy(out=VA[l][kv][0][:, 4, 64:128], in_=VCOL[:]), reads=[MISCr], writes=[VAr[l]])
                        op(DVE, lambda kv=kv: nc.vector.tensor_copy(out=VA[l][kv][1][:, 4, 0:64], in_=VCOL[:]), reads=[MISCr], writes=[VAr[l]])
                combos = [(qb, kv, par) for qb in range(4) for kv in range(2) for par in range(2)]

                def emit_S(ci):
                    qb, kv, par = combos[ci]
                    i0 = (ci % 2) * 2
                    rows = slice(par * 64, (par + 1) * 64)
                    e = (kv * 2 + par) * 2
                    bks = []
                    for w_ in range(2):
                        bank, bres = new_bank()
                        kc0 = (qb + w_) * 128
                        op(PE, lambda bank=bank, kc0=kc0: nc.tensor.matmul(bank, lhsT=KB[l][rows, kv, kc0:kc0 + 128],
                                                                          rhs=Q[rows, kv * 4:kv * 4 + 4, qb * 128:(qb + 1) * 128], start=True, stop=True),
                           reads=[KBr[l]] + Qr[kv * 4:kv * 4 + 4], writes=[bres])
                        bks.append((bank, bres))
                    for w_ in range(2):
                        bank, bres = bks[w_]
                        i = i0 + w_
                        op(ACT, lambda bank=bank, i=i: nc.scalar.activation(out=EX[:, i, :], in_=bank, func=AF.Exp), reads=[bres], writes=[EXr[i]])
                        op(DVE, lambda i=i, w_=w_: nc.vector.tensor_tensor(out=PT[:, i, :], in0=EX[:, i, :], in1=EM[:, e + w_, :], op=ALU.mult),
                           reads=[EXr[i], CONSTr], writes=[PTr[i]])

                def emit_PV(ci):
                    qb, kv, par = combos[ci]
                    i0 = (ci % 2) * 2
                    ri = ci % 2
                    orow = slice(par * 64, (par + 1) * 64)
                    drow = slice((1 - par) * 64, (2 - par) * 64)
                    bank, bres = new_bank()
                    op(PE, lambda: nc.tensor.matmul(bank, lhsT=VA[l][kv][par][:, qb, :], rhs=PT[:, i0, :], start=True, stop=False),
                       reads=[VAr[l], PTr[i0]], writes=[bres], signal=False)
                    op(PE, lambda: nc.tensor.matmul(bank, lhsT=VA[l][kv][par][:, qb + 1, :], rhs=PT[:, i0 + 1, :], start=False, stop=True),
                       reads=[VAr[l], PTr[i0 + 1]], writes=[bres])
                    op(ACT, lambda: nc.scalar.copy(out=RD[orow, ri, :], in_=bank[drow, :]), reads=[bres], writes=[RDr[ri]])
                    for jj in range(4):
                        h = kv * 8 + 2 * jj + par
                        op(DVE, lambda jj=jj, h=h: nc.vector.tensor_scalar(out=RD[orow, ri, jj * 128:(jj + 1) * 128], in0=RD[orow, ri, jj * 128:(jj + 1) * 128],
                                                                           scalar1=ESC[orow, l * 16 + h:l * 16 + h + 1], scalar2=None, op0=ALU.add),
                           reads=[RDr[ri], MISCr], writes=[RDr[ri]])
                    op(DVE, lambda: nc.vector.reciprocal(out=RD[orow, ri, :], in_=RD[orow, ri, :]), reads=[RDr[ri]], writes=[RDr[ri]])
                    op(DVE, lambda: nc.vector.tensor_tensor(out=OT[orow, kv * 4:kv * 4 + 4, qb * 128:(qb + 1) * 128],
                                                            in0=bank[orow, :].rearrange("p (a b) -> p a b", a=4),
                                                            in1=RD[orow, ri, :].rearrange("p (a b) -> p a b", a=4), op=ALU.mult),
                       reads=[bres, RDr[ri]], writes=[OTr])

                def conv_gen():
                    for q4 in range(4):
                        op(ACT, lambda q4=q4: nc.scalar.copy(out=ZZ[:, :, 0:2], in_=CZC[l][:, q4 * 2:q4 * 2 + 2, :]), reads=[CZCr[l]], writes=[ZZh])
                        s1 = slab_cols(w_in[l], 16, C_CC + q4 * 256, 256)
                        wv1 = wview(s1, 16, 256)
                        for j in range(2):
                            bank, bres = new_bank()
                            mm_group(bank, bres, [wv1[:, k, j * 128:(j + 1) * 128] for k in range(16)], [H[:, k, :] for k in range(16)], Hr, extra_reads=[WSr[s1]])
                            op(ACT, lambda j=j, bank=bank: nc.scalar.copy(out=CCt[:, j, :], in_=bank), reads=[bres], writes=[CCtr[j]])
                            yield
                        s2 = slab_cols(w_in[l], 16, C_CH + q4 * 256, 256)
                        wv2 = wview(s2, 16, 256)
                        for j in range(2):
                            c = q4 * 2 + j
                            bank, bres = new_bank()
                            mm_group(bank, bres, [wv2[:, k, j * 128:(j + 1) * 128] for k in range(16)], [H[:, k, :] for k in range(16)], Hr, extra_reads=[WSr[s2]])
                            op(DVE, lambda j=j, bank=bank: nc.vector.tensor_tensor(out=ZZ[:, j, 2:514], in0=CCt[:, j, :], in1=bank, op=ALU.mult),
                               reads=[bres, CCtr[j]], writes=[ZZr[j]])
                            op(ACT, lambda j=j, c=c: nc.scalar.activation(out=CCt[:, j, :], in_=ZZ[:, j, 0:512], func=AF.Identity, scale=pcol(pb + 88 + c)),
                               reads=[ZZr[j], ZZh, CONSTr], writes=[CCtr[j]])
                            op(DVE, lambda j=j, c=c: nc.vector.scalar_tensor_tensor(out=CCt[:, j, :], in0=ZZ[:, j, 1:513], scalar=pcol(pb + 96 + c), in1=CCt[:, j, :],
                                                                                    op0=ALU.mult, op1=ALU.add), reads=[ZZr[j], ZZh, CCtr[j], CONSTr], writes=[CCtr[j]])
                            op(DVE, lambda j=j, c=c: nc.vector.scalar_tensor_tensor(out=CCt[:, j, :], in0=ZZ[:, j, 2:514], scalar=pcol(pb + 104 + c), in1=CCt[:, j, :],
                                                                                    op0=ALU.mult, op1=ALU.add), reads=[ZZr[j], CCtr[j], CONSTr], writes=[CCtr[j]])
                            yield
                        op(ACT, lambda q4=q4: nc.scalar.copy(out=CZC[l][:, q4 * 2:q4 * 2 + 2, :], in_=ZZ[:, :, 512:514]), reads=ZZr, writes=[CZCr[l]])
                        s3 = slab_cols(w_in[l], 16, C_CB + q4 * 256, 256)
                        wv3 = wview(s3, 16, 256)
                        for j in range(2):
                            c = q4 * 2 + j
                            bank, bres = new_bank()
                            mm_group(bank, bres, [wv3[:, k, j * 128:(j + 1) * 128] for k in range(16)], [H[:, k, :] for k in range(16)], Hr, extra_reads=[WSr[s3]])
                            op(DVE, lambda j=j, c=c, bank=bank: nc.vector.tensor_tensor(out=CC[:, c, :], in0=CCt[:, j, :], in1=bank, op=ALU.mult),
                               reads=[bres, CCtr[j]], writes=[CCr[c]])
                            yield

                cg = conv_gen()
                emit_S(0)
                for ci in range(len(combos)):
                    if ci + 1 < len(combos):
                        emit_S(ci + 1)
                    emit_PV(ci)
                    for _ in range(2 if ci % 2 == 0 else 1):
                        next(cg, None)
                for _ in cg:
                    pass
                op(ACT, lambda: nc.scalar.copy(out=KB[l][:, :, 0:128], in_=KB[l][:, :, 512:640]), reads=[KBr[l]], writes=[KBr[l]])
                for kv in range(2):
                    for par in range(2):
                        op(ACT, lambda kv=kv, par=par: nc.scalar.copy(out=VA[l][kv][par][:, 0, :], in_=VA[l][kv][par][:, 4, :]), reads=[VAr[l]], writes=[VAr[l]])
                if ti == 0:
                    for kv in range(2):
                        op(DVE, lambda kv=kv: nc.vector.memset(VA[l][kv][0][:, 4, 64:128], 1.0), reads=[VAr[l]], writes=[VAr[l]])
                        op(DVE, lambda kv=kv: nc.vector.memset(VA[l][kv][1][:, 4, 0:64], 1.0), reads=[VAr[l]], writes=[VAr[l]])
                bury(L)

        def merge_stage(l, ti, cs=slice(0, T)):
            PA, PAr, OT, OTr, CC, CCr = MB['PA'], MB['PAr'], MB['OT'], MB['OTr'], MB['CC'], MB['CCr']
            L = []
            bcs = slice(0, cs.stop - cs.start)
            with nc.sbuf_tensor(un("MG"), [128, NCH, T], BF16) as MG, nc.sbuf_tensor(un("ACC"), [128, 4, T], F32) as ACC, \
                    nc.sbuf_tensor(un("TH"), [128, 2, T], F32) as TH, nc.sbuf_tensor(un("TMP"), [128, 2, T], F32) as TMP:
                MGr = [mk(L) for _ in range(NCH)]
                ACCr = [mk(L) for _ in range(4)]
                THr = [mk(L), mk(L)]
                TMPr = [mk(L), mk(L)]
                srcs = [(PA, PAr, w_pool_out), (OT, [OTr] * 8, w_attn_out), (CC, CCr, w_conv_out)]
                cnt = 0
                for m4 in range(4):
                    for br in range(3):
                        SRC, SRCr, wout = srcs[br]
                        sg = slab_cols(w_in[l], 16, C_G + br * D + m4 * 512, 512)
                        so = slab_cols(wout[l], 8, m4 * 512, 512)
                        wg = wview(sg, 16, 512)
                        wo = wview(so, 8, 512)
                        for mm in range(4):
                            m = m4 * 4 + mm
                            bg, bgr = new_bank()
                            mm_group(bg[:, bcs], bgr, [wg[:, k, mm * 128:(mm + 1) * 128] for k in range(16)], [H[:, k, cs] for k in range(16)], Hr, extra_reads=[WSr[sg]])
                            by, byr = new_bank()
                            mm_group(by[:, bcs], byr, [wo[:, k, mm * 128:(mm + 1) * 128] for k in range(8)], [SRC[:, k, cs] for k in range(8)], SRCr, extra_reads=[WSr[so]])
                            i = cnt % 2
                            cnt += 1
                            op(ACT, lambda bg=bg, i=i, br=br, m=m: nc.scalar.activation(out=TH[:, i, cs], in_=bg[:, bcs], func=AF.Tanh, scale=0.5,
                                                                                        bias=HBG[:, l * 48 + br * 16 + m:l * 48 + br * 16 + m + 1]),
                               reads=[bgr, MISCr], writes=[THr[i]])
                            if br == 0:
                                op(DVE, lambda by=by, i=i, mm=mm: nc.vector.scalar_tensor_tensor(out=ACC[:, mm, cs], in0=TH[:, i, cs], scalar=1.0, in1=by[:, bcs],
                                                                                                 op0=ALU.add, op1=ALU.mult), reads=[THr[i], byr], writes=[ACCr[mm]])
                            else:
                                op(DVE, lambda by=by, i=i: nc.vector.scalar_tensor_tensor(out=TMP[:, i, cs], in0=TH[:, i, cs], scalar=1.0, in1=by[:, bcs],
                                                                                          op0=ALU.add, op1=ALU.mult), reads=[THr[i], byr], writes=[TMPr[i]])
                                if br == 1:
                                    op(DVE, lambda i=i, mm=mm: nc.vector.tensor_tensor(out=ACC[:, mm, cs], in0=ACC[:, mm, cs], in1=TMP[:, i, cs], op=ALU.add),
                                       reads=[ACCr[mm], TMPr[i]], writes=[ACCr[mm]])
                                else:
                                    op(DVE, lambda i=i, mm=mm, m=m: nc.vector.tensor_tensor(out=MG[:, m, cs], in0=ACC[:, mm, cs], in1=TMP[:, i, cs], op=ALU.add),
                                       reads=[ACCr[mm], TMPr[i]], writes=[MGr[m]])
                for mq in range(4):
                    s = slab_cols(w_o[l], 16, mq * 512, 512)
                    wv = wview(s, 16, 512)
                    for mm in range(4):
                        m = mq * 4 + mm
                        bank, bres = new_bank()
                        mm_group(bank[:, bcs], bres, [wv[:, k, mm * 128:(mm + 1) * 128] for k in range(16)], [MG[:, k, cs] for k in range(16)], MGr, extra_reads=[WSr[s]])
                        op(DVE, lambda bank=bank, m=m: nc.vector.scalar_tensor_tensor(out=X[:, m, cs], in0=bank[:, bcs], scalar=0.5, in1=X[:, m, cs],
                                                                                      op0=ALU.mult, op1=ALU.add), reads=[bres, Xr[m]], writes=[Xr[m]])
                bury(L)

        def ffn_stage(l, ti):
            pb = l * LPV
            L = []
            with nc.sbuf_tensor(un("A0"), [128, 12, T], BF16) as A0, nc.sbuf_tensor(un("A1"), [128, 12, T], BF16) as A1, \
                    nc.sbuf_tensor(un("ZB"), [128, 2, 2, 514], F32) as ZB, nc.sbuf_tensor(un("CGV"), [128, 2, 2, T], F32) as CGV, \
                    nc.sbuf_tensor(un("TH2"), [128, 2, T], F32) as TH, nc.sbuf_tensor(un("TMP2"), [128, 2, T], F32) as TMP:
                Ab = [A0, A1]
                Abr = [[mk(L) for _ in range(12)] for _ in range(2)]
                ZBr = [mk(L), mk(L)]
                CGVr = [mk(L), mk(L)]
                THr = [mk(L), mk(L)]
                TMPr = [mk(L), mk(L)]
                groups = [(0, 12), (12, 12), (24, 12), (36, 8)]
                pending = []

                def emit_down(g0, ng, ab, abr):
                    for mq in range(4):
                        src = w_down[l][g0 * 128:(g0 + ng) * 128, :].rearrange("(k p) n -> p k n", p=128)[:, :, mq * 512:(mq + 1) * 512]
                        s = load_slab([(lambda t, ng=ng: t[:, 0:ng * 512].rearrange("p (k n) -> p k n", n=512), src)])
                        wv = wview(s, ng, 512)
                        for mm in range(4):
                            m = mq * 4 + mm
                            bank, bres = new_bank()
                            mm_group(bank, bres, [wv[:, k, mm * 128:(mm + 1) * 128] for k in range(ng)], [ab[:, k, :] for k in range(ng)], abr[:ng], extra_reads=[WSr[s]])
                            op(DVE, lambda bank=bank, m=m: nc.vector.tensor_tensor(out=X[:, m, :], in0=bank, in1=X[:, m, :], op=ALU.add),
                               reads=[bres, Xr[m]], writes=[Xr[m]])

                for gi, (g0, ng) in enumerate(groups):
                    ab = Ab[gi % 2]
                    abr = Abr[gi % 2]
                    for i4 in range(g0 // 2, (g0 + ng) // 2):
                        if pending and i4 == g0 // 2 + 2:
                            emit_down(*pending.pop())
                        upv = w_up[l].rearrange("(k p) n -> p k n", p=128)
                        sg = load_slab([(lambda t: t[:, 0:4096].rearrange("p (k n) -> p k n", n=256), upv[:, :, i4 * 256:(i4 + 1) * 256]),
                                        (lambda t: t[:, 4096:8192].rearrange("p (k n) -> p k n", n=256), upv[:, :, DFF + i4 * 256:DFF + (i4 + 1) * 256])])
                        sv = sg
                        wg = WS[sg][:, 0:4096].rearrange("p (k n) -> p k n", n=256)
                        wv = WS[sg][:, 4096:8192].rearrange("p (k n) -> p k n", n=256)
                        for jj in range(2):
                            j = i4 * 2 + jj
                            jl = j - g0
                            i = j % 2
                            bg, bgr = new_bank()
                            mm_group(bg, bgr, [wg[:, k, jj * 128:(jj + 1) * 128] for k in range(16)], [H[:, k, :] for k in range(16)], Hr, extra_reads=[WSr[sg]])
                            bv, bvr = new_bank()
                            mm_group(bv, bvr, [wv[:, k, jj * 128:(jj + 1) * 128] for k in range(16)], [H[:, k, :] for k in range(16)], Hr, extra_reads=[WSr[sv]])
                            op(DVE, lambda i=i, j=j: nc.vector.tensor_copy(out=ZB[:, i, :, 0:2], in_=ZC[l][:, j, :, :]), reads=[ZCr[l]], writes=[ZBr[i]])
                            op(ACT, lambda i=i, bg=bg: nc.scalar.copy(out=ZB[:, i, 0, 2:514], in_=bg), reads=[bgr, ZBr[i]], writes=[ZBr[i]])
                            op(ACT, lambda i=i, bv=bv: nc.scalar.copy(out=ZB[:, i, 1, 2:514], in_=bv), reads=[bvr, ZBr[i]], writes=[ZBr[i]])
                            op(DVE, lambda i=i, j=j: nc.vector.tensor_copy(out=ZC[l][:, j, :, :], in_=ZB[:, i, :, 512:514]), reads=[ZBr[i]], writes=[ZCr[l]])
                            for s_ in range(2):
                                cj = s_ * NA + j
                                op(ACT, lambda i=i, s_=s_, cj=cj: nc.scalar.activation(out=CGV[:, i, s_, :], in_=ZB[:, i, s_, 0:512], func=AF.Identity,
                                                                                       scale=pcol(pb + 112 + cj), bias=pcol(pb + 376 + cj)),
                                   reads=[ZBr[i], CONSTr], writes=[CGVr[i]])
                                op(DVE, lambda i=i, s_=s_, cj=cj: nc.vector.scalar_tensor_tensor(out=CGV[:, i, s_, :], in0=ZB[:, i, s_, 1:513], scalar=pcol(pb + 112 + 88 + cj),
                                                                                                 in1=CGV[:, i, s_, :], op0=ALU.mult, op1=ALU.add),
                                   reads=[ZBr[i], CGVr[i], CONSTr], writes=[CGVr[i]])
                                op(DVE, lambda i=i, s_=s_, cj=cj: nc.vector.scalar_tensor_tensor(out=CGV[:, i, s_, :], in0=ZB[:, i, s_, 2:514], scalar=pcol(pb + 112 + 176 + cj),
                                                                                                 in1=CGV[:, i, s_, :], op0=ALU.mult, op1=ALU.add),
                                   reads=[ZBr[i], CGVr[i], CONSTr], writes=[CGVr[i]])
                            op(ACT, lambda i=i: nc.scalar.activation(out=TH[:, i, :], in_=CGV[:, i, 0, :], func=AF.Tanh, scale=0.5), reads=[CGVr[i]], writes=[THr[i]])
                            op(DVE, lambda i=i: nc.vector.scalar_tensor_tensor(out=TMP[:, i, :], in0=TH[:, i, :], scalar=1.0, in1=CGV[:, i, 0, :],
                                                                               op0=ALU.add, op1=ALU.mult), reads=[THr[i], CGVr[i]], writes=[TMPr[i]])
                            op(DVE, lambda i=i, jl=jl, ab=ab: nc.vector.scalar_tensor_tensor(out=ab[:, jl, :], in0=TMP[:, i, :], scalar=0.5, in1=CGV[:, i, 1, :],
                                                                                             op0=ALU.mult, op1=ALU.mult), reads=[TMPr[i], CGVr[i]], writes=[abr[jl]])
                    pending.append((g0, ng, ab, abr))
                while pending:
                    emit_down(*pending.pop())
                bury(L)

        def ffn_carry_only(l):
            cs = slice(T - 128, T)
            for i4 in range(NA // 4):
                sg = slab_cols(w_up[l], 16, i4 * 512, 512)
                sv = slab_cols(w_up[l], 16, DFF + i4 * 512, 512)
                wg = wview(sg, 16, 512)
                wv = wview(sv, 16, 512)
                for jj in range(4):
                    j = i4 * 4 + jj
                    bg, bgr = new_bank()
                    mm_group(bg[:, 0:128], bgr, [wg[:, k, jj * 128:(jj + 1) * 128] for k in range(16)], [H[:, k, cs] for k in range(16)], Hr, extra_reads=[WSr[sg]])
                    bv, bvr = new_bank()
                    mm_group(bv[:, 0:128], bvr, [wv[:, k, jj * 128:(jj + 1) * 128] for k in range(16)], [H[:, k, cs] for k in range(16)], Hr, extra_reads=[WSr[sv]])
                    op(ACT, lambda j=j, bg=bg: nc.scalar.copy(out=ZC[l][:, j, 0, :], in_=bg[:, 126:128]), reads=[bgr], writes=[ZCr[l]])
                    op(ACT, lambda j=j, bv=bv: nc.scalar.copy(out=ZC[l][:, j, 1, :], in_=bv[:, 126:128]), reads=[bvr], writes=[ZCr[l]])

        xv = xT.rearrange("(c p) t -> p c t", p=128)
        ov = outT.rearrange("(c p) t -> p c t", p=128)
        for ti in range(ntile):
            for c in range(NCH):
                dma(SP, xsem[c], lambda ti=ti, c=c: nc.sync.dma_start(out=X[:, c, :], in_=xv[:, c, ti * T:(ti + 1) * T]), writes=[Xr[c]])
            for l in range(DEPTH):
                pb = l * LPV
                trimmed = (ti == 0 and l == DEPTH - 1)
                cs = slice(T - 128, T) if trimmed else slice(0, T)
                norm_stage(pb + 0)
                MBL = []
                with nc.sbuf_tensor(un("PA"), [128, 8, T], BF16) as PA_, nc.sbuf_tensor(un("OT"), [128, 8, T], BF16) as OT_, \
                        nc.sbuf_tensor(un("CC"), [128, 8, T], BF16) as CC_:
                    MB.update(PA=PA_, OT=OT_, CC=CC_, PAr=[mk(MBL) for _ in range(8)], OTr=mk(MBL), CCr=[mk(MBL) for _ in range(8)])
                    pool_stage(l, ti)
                    attn_conv_stage(l, ti)
                    merge_stage(l, ti, cs)
                    bury(MBL)
                norm_stage(pb + 16, cs=cs)
                if trimmed:
                    ffn_carry_only(l)
                else:
                    ffn_stage(l, ti)
                if ti == 0 and l == 0:
                    for c in range(NCH):
                        op(DVE, lambda c=c: nc.vector.tensor_tensor(out=X[:, c, :], in0=X[:, c, :], in1=VM[:], op=ALU.mult),
                           reads=[Xr[c], CONSTr], writes=[Xr[c]])
            if ti >= 1:
                norm_stage(2 * LPV, to_x=True, store_ti=ti)
        for c in range(NCH):
            nc.sync.wait_ge(osem[c].sem, osem[c].count)
        fence()
    return nc


_NT = NTILE
_CORES = list(range(8))


def _consts():
    em = np.zeros((128, 8, 4, 128), np.float64)
    j = np.arange(128)[:, None]
    i = np.arange(128)[None, :]
    for kv in range(2):
        for par in range(2):
            for jj in range(4):
                h = kv * 8 + 2 * jj + par
                slope = 2.0 ** (-8.0 * (h + 1) / 16.0)
                for w_ in range(2):
                    dist = (i + 128 - j) if w_ == 0 else (i - j)
                    valid = (dist >= 0) & (dist < 128)
                    em[:, (kv * 2 + par) * 2 + w_, jj, :] = np.where(valid, np.exp(-slope * dist), 0.0)
    return em.reshape(128, 8, 512).astype(np.float32)


def kernel(x, g_mix, w_in, b_gate, w_pool_grp, pool_scale, w_pool_out, sinks, w_attn_out,
           w_conv_mix, w_conv_out, w_o, g_ffn, w_up, w_ffn_conv, b_ffn_conv, w_down, g_final):
    f = lambda a: np.ascontiguousarray(np.asarray(a, dtype=np.float32))
    x = f(x)
    pv = np.zeros((128, NPV), np.float32)

    def cols(v):
        return np.asarray(v, np.float32).reshape(-1, 128).T
    for l in range(DEPTH):
        b = l * LPV
        pv[:, b:b + 16] = cols(g_mix[l])
        pv[:, b + 16:b + 32] = cols(g_ffn[l])
        pv[:, b + 32:b + 80] = cols(b_gate[l])
        pv[:, b + 80:b + 88] = cols(pool_scale[l])
        for k in range(3):
            pv[:, b + 88 + k * 8:b + 96 + k * 8] = cols(w_conv_mix[l][k])
            pv[:, b + 112 + k * 88:b + 200 + k * 88] = cols(w_ffn_conv[l][k])
        pv[:, b + 376:b + 464] = cols(b_ffn_conv[l])
        pv[:, b + 464:b + 480] = np.broadcast_to(np.asarray(sinks[l], np.float32)[None, :], (128, 16))
    pv[:, 2 * LPV:2 * LPV + 16] = cols(g_final)
    emat = _consts()
    ident = np.eye(128, dtype=np.float32).astype(ml_dtypes.bfloat16)
    wts = dict(w_in=f(w_in), w_pool_grp=f(w_pool_grp), w_pool_out=f(w_pool_out), w_attn_out=f(w_attn_out),
               w_conv_out=f(w_conv_out), w_o=f(w_o), w_up=f(w_up), w_down=f(w_down))
    in_maps = []
    for c in _CORES:
        b, q = c // 4, c % 4
        a = q * 4096
        xt = np.zeros((D, NTOK), np.float32)
        lo = a - T
        if lo < 0:
            xt[:, T:] = x[b, 0:4096, :].T
        else:
            xt[:, :] = x[b, lo:a + 4096, :].T
        rc = np.zeros((128, 8, 16), np.float32)
        for g in range(4):
            w = 2 << g
            if q == 0:
                rc[:, 2 * g:2 * g + 2, :] = (1.0 / np.minimum(np.arange(16) + 1, w)).astype(np.float32)[None, None, :]
            else:
                rc[:, 2 * g:2 * g + 2, :] = np.float32(1.0 / w)
        vm = np.full((128, 512), 0.0 if q == 0 else 1.0, np.float32)
        m = dict(xT=xt, pvec=pv, emat=emat, poolrc=rc, vmask=vm, ident=ident)
        m.update(wts)
        in_maps.append(m)
    nc = build_program(_NT)
    res = run_bass_kernel_spmd(nc, in_maps, core_ids=list(range(len(_CORES))))
    out = np.zeros((2, SEQ, D), np.float32)
    n = (_NT - 1) * T
    for i, c in enumerate(_CORES):
        b, q = c // 4, c % 4
        out[b, q * 4096:q * 4096 + n, :] = res.results[i]["outT"].T
    return out
```
